# Optimizing a Trainium2 kernel written in Bass

```python
import math
import jax, jax.numpy as jnp
from jax import lax
import numpy as np

D_MODEL = 1024
BATCH = 16
SEQ = 4096
DEPTH = 1

CHUNK = 64
PLE_DIM = 256
ROPE_THETA = 10000.0
NORM_EPS = 1e-6

GLA_HEADS = 4
GLA_WIDTH = D_MODEL // 2
GLA_DK = GLA_WIDTH // 2 // GLA_HEADS
GLA_DV = GLA_WIDTH // GLA_HEADS
GLA_GATE_RANK = 16
GLA_GATE_NORM = 16.0

DIFF_HEADS = 4
DIFF_WIDTH = D_MODEL - GLA_WIDTH
DIFF_DV = DIFF_WIDTH // DIFF_HEADS
DIFF_DQK = DIFF_DV // 2
Q_BLOCK = 128

D_FF = 256 * ((8 * D_MODEL // 3 + 255) // 256)
CONV_W = 3

IN_SIZES = (GLA_HEADS * GLA_DK, GLA_HEADS * GLA_DK, GLA_WIDTH, GLA_WIDTH, GLA_GATE_RANK,
            DIFF_WIDTH, DIFF_WIDTH, DIFF_WIDTH)
IN_COLS = sum(IN_SIZES)

kernel_name = "hymba_gla_diffattn_convffn_ple"


def rms_norm(x, g):
    xf = x.astype(jnp.float32)
    y = xf * lax.rsqrt(jnp.mean(xf * xf, axis=-1, keepdims=True) + NORM_EPS)
    return (y * g.astype(jnp.float32)).astype(x.dtype)


def rope(x, pos):
    d = x.shape[-1]
    inv_freq = ROPE_THETA ** (-jnp.arange(0, d, 2, dtype=jnp.float32) / d)
    ang = pos.astype(jnp.float32)[..., None] * inv_freq
    cos, sin = jnp.cos(ang)[:, :, None, :], jnp.sin(ang)[:, :, None, :]
    xf = x.astype(jnp.float32)
    x1, x2 = xf[..., : d // 2], xf[..., d // 2:]
    return jnp.concatenate([x1 * cos - x2 * sin, x2 * cos + x1 * sin], axis=-1).astype(x.dtype)


def gla_mixer(q, k, v, g, a_low, w_a_up, b_a, norm_g):
    B, T = q.shape[:2]
    N = T // CHUNK
    H, dk, dv = GLA_HEADS, GLA_DK, GLA_DV
    qf = q.astype(jnp.float32).reshape(B, N, CHUNK, H, dk) * (dk ** -0.5)
    kf = k.astype(jnp.float32).reshape(B, N, CHUNK, H, dk)
    vf = v.astype(jnp.float32).reshape(B, N, CHUNK, H, dv)
    log_a = jax.nn.log_sigmoid((a_low @ w_a_up + b_a).astype(jnp.float32)) / GLA_GATE_NORM
    b = jnp.cumsum(log_a.reshape(B, N, CHUNK, H, dk), axis=2)
    b_last = b[:, :, -1:]
    eb, enb = jnp.exp(b), jnp.exp(-b)
    q_fwd = qf * eb
    a_fwd = jnp.einsum('bnthd,bnshd->bnhts', q_fwd, kf * enb)
    a_bwd = jnp.einsum('bnthd,bnshd->bnhts', qf * enb, kf * eb)
    tri = jnp.tril(jnp.ones((CHUNK, CHUNK), dtype=bool))
    a = jnp.where(tri, a_fwd, a_bwd)
    o_intra = jnp.einsum('bnhts,bnshv->bnthv', a, vf)
    d_state = jnp.einsum('bnshk,bnshv->bnhkv', kf * jnp.exp(b_last - b), vf)
    decay = jnp.exp(b_last[:, :, 0])

    def step(state, inp):
        dec, ds = inp
        return dec[..., None] * state + ds, state

    s0 = jnp.zeros((B, H, dk, dv), jnp.float32)
    _, s_prev = lax.scan(step, s0, (jnp.moveaxis(decay, 1, 0), jnp.moveaxis(d_state, 1, 0)))
    s_prev = jnp.moveaxis(s_prev, 0, 1)
    o_inter = jnp.einsum('bnthk,bnhkv->bnthv', q_fwd, s_prev)
    o = (o_intra + o_inter).reshape(B, T, H, dv)
    o = rms_norm(o, norm_g.reshape(H, dv)).reshape(B, T, H * dv)
    return (o * jax.nn.silu(g.astype(jnp.float32))).astype(q.dtype)


def diff_attention(q, k, v, pos, lam, lam_init, norm_g):
    B, T = q.shape[:2]
    H, dqk, dv = DIFF_HEADS, DIFF_DQK, DIFF_DV
    q = (rope(q.reshape(B, T, H * 2, dqk), pos) * (dqk ** -0.5)).reshape(B, T, H, 2, dqk)
    k = rope(k.reshape(B, T, H * 2, dqk), pos).reshape(B, T, H, 2, dqk)
    v = v.reshape(B, T, H, dv)
    outs = []
    for blk in range(T // Q_BLOCK):
        q0 = blk * Q_BLOCK
        k_end = q0 + Q_BLOCK
        s = jnp.einsum('bqhmd,bkhmd->bhmqk', q[:, q0:k_end], k[:, :k_end]).astype(jnp.float32)
        q_chunk = (q0 + jnp.arange(Q_BLOCK)) // CHUNK
        k_chunk = jnp.arange(k_end) // CHUNK
        mask = k_chunk[None, :] <= q_chunk[:, None]
        s = jnp.where(mask, s, jnp.float32(-1e30))
        pr = jax.nn.softmax(s, axis=-1)
        w = pr[:, :, 0] - lam * pr[:, :, 1]
        outs.append(jnp.einsum('bhqk,bkhd->bqhd', w.astype(v.dtype), v[:, :k_end]))
    o = jnp.concatenate(outs, axis=1)
    o = rms_norm(o, norm_g.reshape(H, dv)) * (1.0 - lam_init)
    return o.reshape(B, T, H * dv)


def conv_ffn(h, w_up, conv_w, conv_b, w_down):
    T = h.shape[1]
    u = h @ w_up
    up = jnp.pad(u, ((0, 0), (CONV_W - 1, 0), (0, 0)))
    c = conv_b + sum(up[:, j:j + T] * conv_w[j] for j in range(CONV_W))
    gate, val = jnp.split(c, 2, axis=-1)
    return (jax.nn.gelu(gate) * val) @ w_down


def setup_inputs(seed: int = 0) -> dict:
    key = jax.random.key(seed)
    ks = jax.random.split(key, 24)
    f32 = jnp.float32
    nrm = lambda k, shape, s: jax.random.normal(k, shape, f32) * s
    gain = lambda k, shape: 1.0 + 0.01 * jax.random.normal(k, shape, f32)
    x = jax.random.normal(ks[0], (BATCH, SEQ, D_MODEL), f32)
    p = jax.random.normal(ks[1], (DEPTH, BATCH, SEQ, PLE_DIM), f32)
    offsets = jax.random.randint(ks[2], (BATCH, 1), 0, 64, dtype=jnp.int32) * CHUNK
    positions = jnp.arange(SEQ, dtype=jnp.int32)[None, :] + offsets
    return {
        "x": x,
        "p": p,
        "positions": positions,
        "norm_mix": gain(ks[3], (DEPTH, D_MODEL)),
        "w_in": nrm(ks[4], (DEPTH, D_MODEL, IN_COLS), D_MODEL ** -0.5),
        "w_a_up": nrm(ks[5], (DEPTH, GLA_GATE_RANK, GLA_HEADS * GLA_DK), GLA_GATE_RANK ** -0.5),
        "b_a": nrm(ks[6], (DEPTH, GLA_HEADS * GLA_DK), 0.01),
        "gla_norm": gain(ks[7], (DEPTH, GLA_WIDTH)),
        "lam_q1": nrm(ks[8], (DEPTH, DIFF_DQK), 0.1),
        "lam_k1": nrm(ks[9], (DEPTH, DIFF_DQK), 0.1),
        "lam_q2": nrm(ks[10], (DEPTH, DIFF_DQK), 0.1),
        "lam_k2": nrm(ks[11], (DEPTH, DIFF_DQK), 0.1),
        "diff_norm": gain(ks[12], (DEPTH, DIFF_WIDTH)),
        "w_out": nrm(ks[13], (DEPTH, D_MODEL, D_MODEL), D_MODEL ** -0.5),
        "norm_ffn": gain(ks[14], (DEPTH, D_MODEL)),
        "w_up": nrm(ks[15], (DEPTH, D_MODEL, 2 * D_FF), D_MODEL ** -0.5),
        "conv_w": nrm(ks[16], (DEPTH, CONV_W, 2 * D_FF), CONV_W ** -0.5),
        "conv_b": nrm(ks[17], (DEPTH, 2 * D_FF), 0.01),
        "w_down": nrm(ks[18], (DEPTH, D_FF, D_MODEL), D_FF ** -0.5),
        "norm_ple": gain(ks[19], (DEPTH, D_MODEL)),
        "w_ple_gate": nrm(ks[20], (DEPTH, D_MODEL, D_MODEL), D_MODEL ** -0.5),
        "w_ple_proj": nrm(ks[21], (DEPTH, PLE_DIM, D_MODEL), PLE_DIM ** -0.5),
        "norm_final": gain(ks[22], (D_MODEL,)),
    }


def reference(x, p, positions, norm_mix, w_in, w_a_up, b_a, gla_norm, lam_q1, lam_k1, lam_q2, lam_k2,
              diff_norm, w_out, norm_ffn, w_up, conv_w, conv_b, w_down, norm_ple, w_ple_gate,
              w_ple_proj, norm_final):
    split_points = [int(s) for s in np.cumsum(IN_SIZES)[:-1]]
    h = x
    for i in range(DEPTH):
        u = rms_norm(h, norm_mix[i])
        z = u @ w_in[i]
        gq, gk, gv, gg, ga, dq, dk, dv = jnp.split(z, split_points, axis=-1)
        o_gla = gla_mixer(gq, gk, gv, gg, ga, w_a_up[i], b_a[i], gla_norm[i])
        lam_init = 0.8 - 0.6 * math.exp(-0.3 * i)
        lam = (jnp.exp(jnp.sum(lam_q1[i].astype(jnp.float32) * lam_k1[i].astype(jnp.float32)))
               - jnp.exp(jnp.sum(lam_q2[i].astype(jnp.float32) * lam_k2[i].astype(jnp.float32)))
               + lam_init)
        o_diff = diff_attention(dq, dk, dv, positions, lam, lam_init, diff_norm[i])
        h = h + jnp.concatenate([o_gla, o_diff], axis=-1) @ w_out[i]
        h = h + conv_ffn(rms_norm(h, norm_ffn[i]), w_up[i], conv_w[i], conv_b[i], w_down[i])
        gate = jax.nn.sigmoid(rms_norm(h, norm_ple[i]) @ w_ple_gate[i])
        h = h + gate * (p[i] @ w_ple_proj[i])
    return rms_norm(h, norm_final)
```

```python
import bisect
import contextlib
import math
import numpy as np
import concourse.bass as bass
import concourse.mybir as mybir
from concourse.bass_utils import run_bass_kernel_spmd

F32 = mybir.dt.float32
BF16 = mybir.dt.bfloat16
I32 = mybir.dt.int32
AF = mybir.ActivationFunctionType
ALU = mybir.AluOpType

COMPUTE = ("pe", "act", "dve", "pool")
MARKS = []
NCORES = 8
SEQ = 4096
NSEQ = 2
T = 512
NTI = SEQ // T
EPS = 1e-6
DFF = 2816
NSLOT = 4


class _Rec:
    def __getattr__(self, name):
        def f(*a, **k):
            self.call = (name, a, k)
            return self
        return f


class Prog:
    def __init__(self, nc):
        self.nc = nc
        self.streams = {s: [] for s in ("pe", "act", "dve", "pool", "sp")}
        self.count = {}
        self.res = {}
        self.seen = {s: {} for s in self.streams}
        self.need = {}
        self.dma_chans = set()

    def _st(self, key):
        st = self.res.get(key)
        if st is None:
            st = self.res[key] = {"w": None, "r": []}
        return st

    def add(self, stream, fn, reads=(), writes=(), chan=None):
        rec = _Rec()
        fn(rec)
        call = rec.call
        fn = lambda e, call=call: getattr(e, call[0])(*call[1], **call[2])
        if chan is None:
            chan = stream
        else:
            self.dma_chans.add(chan)
        idx = self.count.get(chan, 0) + 1
        self.count[chan] = idx
        me = (chan, idx)
        deps = set()
        for key in reads:
            st = self._st(key)
            if st["w"] is not None:
                deps.add(st["w"])
            if isinstance(key, tuple) and key[0] == "ps":
                for r in st["r"]:
                    if r[0] != chan:
                        deps.add(r)
        for key in writes:
            st = self._st(key)
            if st["w"] is not None:
                deps.add(st["w"])
            deps.update(st["r"])
        for key in reads:
            self._st(key)["r"].append(me)
        for key in writes:
            st = self._st(key)
            st["w"] = me
            st["r"] = []
        waits = []
        seen = self.seen[stream]
        best = {}
        for (c, n) in deps:
            if c == chan:
                if chan == "pe":
                    continue
                if chan in COMPUTE and idx - n > 1:
                    continue
            if n > best.get(c, 0):
                best[c] = n
        for c in sorted(best):
            n = best[c]
            if seen.get(c, 0) >= n:
                continue
            seen[c] = n
            waits.append((c, n))
            self.need.setdefault(c, set()).add(n)
        self.streams[stream].append((fn, waits, chan, idx))
        return me

    def wait_all(self, stream, chans):
        w = []
        for c in chans:
            n = self.count.get(c, 0)
            if n == 0:
                continue
            w.append((c, n))
            self.need.setdefault(c, set()).add(n)
            self.seen[stream][c] = max(self.seen[stream].get(c, 0), n)
        self.streams[stream].append((None, w, None, None))

    def emit(self):
        nc = self.nc
        chans = sorted(set(self.count.keys()))
        sems = {}
        with contextlib.ExitStack() as es:
            for c in chans:
                sems[c] = es.enter_context(nc.semaphore("s_" + c))
            needl = {c: sorted(v) for c, v in self.need.items()}

            def val(c, n):
                if c in self.dma_chans:
                    return 16 * n
                return bisect.bisect_right(needl[c], n)

            def run(stream, eng):
                for (fn, waits, chan, idx) in self.streams[stream]:
                    for (c, n) in waits:
                        v = val(c, n)
                        if v > 0:
                            eng.wait_ge(sems[c], v)
                    if fn is None:
                        continue
                    ins = fn(eng)
                    if chan in self.dma_chans:
                        ins.then_inc(sems[chan], 16)
                    elif idx in self.need.get(chan, ()):
                        ins.then_inc(sems[chan], 1)

            with nc.Block() as block:
                @block.sync
                def _(e):
                    run("sp", e)

                @block.scalar
                def _(e):
                    run("act", e)

                @block.tensor
                def _(e):
                    run("pe", e)

                @block.vector
                def _(e):
                    run("dve", e)

                @block.gpsimd
                def _(e):
                    run("pool", e)


def host_consts():
    s = np.arange(128)[:, None]
    t = np.arange(128)[None, :]
    same = (s // 64) == (t // 64)
    mf = (same & (s <= t)).astype(np.float32)
    mb = (same & (s > t)).astype(np.float32)
    pidx = np.arange(128)
    perm = np.zeros((128, 128), np.float32)
    src = np.where((pidx % 64) < 32, pidx + 32, pidx - 32)
    perm[src, pidx] = 1.0
    j = (pidx % 32).astype(np.float32)
    invf = (np.float32(10000.0) ** (-(2.0 * j) / np.float32(64.0))).astype(np.float32)
    sgn = np.where((pidx % 64) < 32, -1.0, 1.0).astype(np.float32)
    return {
        "c_ident": np.eye(128, dtype=np.float32),
        "c_perm": perm,
        "c_mf": mf, "c_mb": mb,
        "c_mfs": (-mf / 16.0).astype(np.float32), "c_mbs": (-mb / 16.0).astype(np.float32),
        "c_invf": invf.reshape(128, 1), "c_sgn": sgn.reshape(128, 1),
    }


def build_program(ntiles_total=NSEQ * NTI, dbg=False):
    nc = bass.Bass("TRN2", target_bir_lowering=False)

    def din(name, shape, dt=F32):
        return nc.dram_tensor(name, shape, dt, kind="ExternalInput").ap()

    NTOK = NSEQ * SEQ
    x_d = din("x", [NTOK, 1024])
    p_d = din("p", [NTOK, 256])
    pos_d = din("pos", [1, NTOK], I32)
    w_in_d = din("w_in", [1024, 3088])
    w_out_d = din("w_out", [1024, 1024])
    w_up_d = din("w_up", [1024, 5632])
    w_dn_d = din("w_down", [DFF, 1024])
    w_pg_d = din("w_ple_gate", [1024, 1024])
    w_pp_d = din("w_ple_proj", [256, 1024])
    w_au_d = din("w_a_up", [16, 256])
    b_a_d = din("b_a", [1, 256])
    n_mix_d = din("norm_mix", [1, 1024])
    n_ffn_d = din("norm_ffn", [1, 1024])
    n_ple_d = din("norm_ple", [1, 1024])
    n_fin_d = din("norm_final", [1, 1024])
    g_norm_d = din("gla_norm", [1, 512])
    d_norm_d = din("diff_norm", [1, 512])
    cw_d = din("conv_w", [3, 5632])
    cb_d = din("conv_b", [1, 5632])
    lam_d = [din(n, [1, 64]) for n in ("lam_q1", "lam_k1", "lam_q2", "lam_k2")]
    c_ident_d = din("c_ident", [128, 128])
    c_perm_d = din("c_perm", [128, 128])
    c_mf_d = din("c_mf", [128, 128])
    c_mb_d = din("c_mb", [128, 128])
    c_mfs_d = din("c_mfs", [128, 128])
    c_mbs_d = din("c_mbs", [128, 128])
    c_invf_d = din("c_invf", [128, 1])
    c_sgn_d = din("c_sgn", [128, 1])
    y_d = nc.dram_tensor("y", [NTOK, 1024], F32, kind="ExternalOutput").ap()

    win_bf = nc.dram_tensor("win_bf", [1024, 3088], BF16, kind="Internal").ap()
    wout_bf = nc.dram_tensor("wout_bf", [1024, 1024], BF16, kind="Internal").ap()
    wup_bf = nc.dram_tensor("wup_bf", [1024, 5632], BF16, kind="Internal").ap()
    wdn_bf = nc.dram_tensor("wdn_bf", [DFF, 1024], BF16, kind="Internal").ap()
    wpg_bf = nc.dram_tensor("wpg_bf", [1024, 1024], BF16, kind="Internal").ap()

    if dbg:
        d_oT = nc.dram_tensor("d_oT", [128, 8 * T], F32, kind="ExternalOutput").ap()
        d_h = nc.dram_tensor("d_h", [128, 4, 1024], F32, kind="ExternalOutput").ap()
        d_qk = nc.dram_tensor("d_qk", [128, 5 * T], F32, kind="ExternalOutput").ap()
        d_sp = nc.dram_tensor("d_sp", [128, 4, 256], F32, kind="ExternalOutput").ap()
        d_g = nc.dram_tensor("d_g", [128, 8 * T], F32, kind="ExternalOutput").ap()
        d_gv = nc.dram_tensor("d_gv", [128, 4 * 512], F32, kind="ExternalOutput").ap()
    P = Prog(nc)
    with contextlib.ExitStack() as es:
        def sb(name, shape, dt=F32):
            return es.enter_context(nc.sbuf_tensor(name, shape, dt))

        pb2 = [es.enter_context(nc.psum_tensor("pb%d" % i, [128, 1024], F32)) for i in range(4)]
        pbk = []
        for i in range(4):
            pbk.append(pb2[i][:, 0:512])
            pbk.append(pb2[i][:, 512:1024])

        def PS(b):
            return ("ps", b)

        h = sb("h", [128, 4, 1024])
        KT = sb("KT", [128, 4, SEQ], BF16)
        V = sb("V", [128, SEQ // 128, 512], BF16)
        actF = sb("actT", [128, 22 * T], BF16)
        actT = actF[:, :].rearrange("p (j t) -> p j t", j=22)
        hnT = sb("hnT", [128, 8, T], BF16)
        oTf = sb("oT", [128, 8 * T], BF16)
        oT = oTf[:, :].rearrange("p (k t) -> p k t", k=8)
        qT = sb("qT", [128, 4, T], BF16)
        ring = [sb("ring%d" % i, [128, 2048], BF16) for i in range(NSLOT)]
        wga = sb("wga", [128, 8, 16], BF16)
        wpp = sb("wpp", [128, 2, 1024], BF16)
        wau = sb("wau", [16, 256], BF16)
        HB = sb("HB", [128, 44, 2])
        gf_bc = sb("gf_bc", [128, 1024])
        ba_bc = sb("ba_bc", [128, 256])
        gT1 = sb("gT1", [128, 8]); gT2 = sb("gT2", [128, 8]); gT3 = sb("gT3", [128, 8])
        gnT = sb("gnT", [128, 4]); dnT = sb("dnT", [128, 4]); dn8 = sb("dn8", [128, 4])
        cwT = sb("cwT", [128, 3, 44]); cbT = sb("cbT", [128, 44])
        lamv = [sb("lamv%d" % i, [128, 64]) for i in range(4)]
        lamt = sb("lamt", [128, 64]); lams = sb("lams", [128, 2]); lame = sb("lame", [128, 2]); nlam = sb("nlam", [128, 1])
        ident_bf = sb("ident_bf", [128, 128], BF16)
        perm_bf = sb("perm_bf", [128, 128], BF16)
        mf = sb("mf", [128, 128]); mb = sb("mb", [128, 128]); mfs = sb("mfs", [128, 128]); mbs = sb("mbs", [128, 128])
        ones_bf = sb("ones_bf", [128, 128], BF16)
        ones_f = sb("ones_f", [128, 128])
        invf = sb("invf", [128, 1]); sgn = sb("sgn", [128, 1])
        marker = sb("marker", [128, 1])
        ssq = sb("ssq", [128, 4]); lnv = sb("lnv", [128, 4]); rstd = sb("rstd", [128, 4])
        hn_tok = [sb("hn_tok0", [128, 1024], BF16), sb("hn_tok1", [128, 1024], BF16)]
        junk = hn_tok[1]
        posf = sb("posf", [128, T])
        cosT = oTf[:, 4 * T:6 * T].bitcast(F32)
        sinT = oTf[:, 6 * T:8 * T].bitcast(F32)
        p_bf = sb("p_bf", [128, 4, 256], BF16)
        pT = qT[:, 0:2, :]
        sg = sb("sg", [128, 4, T], BF16)
        aT = sb("aT", [16, T], BF16)
        enb = sb("enb", [128, 2, T], BF16)
        dec = sb("dec", [128, 2, 8])
        def av(a, n, shp):
            return actF[:, a:a + n].rearrange("p (j t) -> p j t", j=shp)
        qf = av(0, 1024, 2); qe = av(1024, 1024, 2); ke = av(2048, 1024, 2); kb = av(3072, 1024, 2)
        gv_tok = av(4096, 2048, 4)
        eb = av(10240, 1024, 2)
        spv = actF[:, 6144:8192].bitcast(F32).rearrange("p (s n) -> p s n", s=4)
        edec = actF[:, 8192:10240].bitcast(F32).rearrange("p (s n) -> p s n", s=4)
        kdecP = sb("kdecP", [128, 4, 4, 128], BF16)
        Abf = [sb("Abf%d" % i, [128, 4, 128], BF16) for i in range(2)]
        S = sb("S", [128, 2, 2, 128]); Sbf = sb("Sbf", [128, 9, 2, 128], BF16)
        fs = [sb("fs%d" % i, [128, T]) for i in range(8)]
        Pb = [sb("Pb%d" % i, [128, 2, T], BF16) for i in range(2)]
        bs = [Pb[0][:, 0, :], Pb[0][:, 1, :], Pb[1][:, 0, :], Pb[1][:, 1, :], sb("bs4", [128, T], BF16), sb("bs5", [128, T], BF16)]
        Ub = [sb("Ub%d" % i, [128, T + 2]) for i in range(2)]

        trv = [pbk[6][:].bitcast(BF16), pbk[7][:].bitcast(BF16)]

        ccount = [0]

        def cload(dst, src, key, **kw):
            ccount[0] += 1
            P.add("sp", lambda e: e.dma_start(out=dst, in_=src, **kw), writes=["c%d" % ccount[0]], chan="q_const")

        cload(mf[:], c_mf_d, "c"); cload(mb[:], c_mb_d, "c"); cload(mfs[:], c_mfs_d, "c"); cload(mbs[:], c_mbs_d, "c")
        cload(invf[:], c_invf_d, "c"); cload(sgn[:], c_sgn_d, "c")
        cload(gf_bc[:], n_fin_d.partition_broadcast(128), "c")
        cload(ba_bc[:], b_a_d.partition_broadcast(128), "c")
        for gt, d in ((gT1, n_mix_d), (gT2, n_ffn_d), (gT3, n_ple_d)):
            cload(gt[:], d.rearrange("o (kc p) -> p (o kc)", p=128), "c", allow_slow_non_contiguous=True)
        cload(gnT[:], g_norm_d.rearrange("o (h p) -> p (o h)", p=128), "c", allow_slow_non_contiguous=True)
        cload(dnT[:], d_norm_d.rearrange("o (h p) -> p (o h)", p=128), "c", allow_slow_non_contiguous=True)
        cload(cwT[:], cw_d.rearrange("j (c p) -> p j c", p=128), "c", allow_slow_non_contiguous=True)
        cload(cbT[:], cb_d.rearrange("o (c p) -> p (o c)", p=128), "c", allow_slow_non_contiguous=True)
        for i in range(4):
            cload(lamv[i][:], lam_d[i].partition_broadcast(128), "c")

        def castdma(dst, src):
            P.add("pool", lambda e: e.dma_start(out=dst, in_=src), chan="q_cast")

        for k in range(8):
            r0, r1 = k * 128, (k + 1) * 128
            castdma(win_bf[r0:r1, :], w_in_d[r0:r1, :])
            castdma(wup_bf[r0:r1, :], w_up_d[r0:r1, :])
            castdma(wout_bf[r0:r1, :], w_out_d[r0:r1, :])
            castdma(wpg_bf[r0:r1, :], w_pg_d[r0:r1, :])
        for k in range(22):
            r0, r1 = k * 128, (k + 1) * 128
            castdma(wdn_bf[r0:r1, :], w_dn_d[r0:r1, :])
        castdma(wpp[:], w_pp_d.rearrange("(kc p) n -> p kc n", p=128))
        castdma(wau[:], w_au_d)
        castdma(wga[:], w_in_d.rearrange("(kc p) n -> p kc n", p=128)[:, :, 1536:1552])
        castdma(ident_bf[:], c_ident_d)
        castdma(perm_bf[:], c_perm_d)
        P.wait_all("pool", ["q_cast", "q_const"])
        P.add("pool", lambda e: e.memset(marker[:], 0.0), writes=["setup"])
        SETUP = ["setup"]
        P.add("pool", lambda e: e.memset(ones_bf[:], 1.0), writes=["ones_bf"])
        P.add("pool", lambda e: e.memset(ones_f[:], 1.0), writes=["ones_f"])
        P.add("dve", lambda e: e.tensor_scalar(out=dn8[:], in0=dnT[:], scalar1=0.8, scalar2=None, op0=ALU.mult), reads=SETUP, writes=["dn8"])
        P.add("dve", lambda e: e.tensor_scalar(out=cwT[:, :, 22:44], in0=cwT[:, :, 22:44], scalar1=0.5, scalar2=None, op0=ALU.mult), reads=SETUP, writes=["cwh"])
        P.add("dve", lambda e: e.tensor_scalar(out=cbT[:, 22:44], in0=cbT[:, 22:44], scalar1=0.5, scalar2=None, op0=ALU.mult), reads=SETUP, writes=["cwh"])
        for i in range(2):
            P.add("dve", lambda e, i=i: e.tensor_tensor(out=lamt[:], in0=lamv[2 * i][:], in1=lamv[2 * i + 1][:], op=ALU.mult), reads=SETUP, writes=["lamt"])
            P.add("dve", lambda e, i=i: e.tensor_reduce(out=lams[:, i:i + 1], in_=lamt[:], axis=mybir.AxisListType.X, op=ALU.add), reads=["lamt"], writes=["lams"])
        P.add("act", lambda e: e.activation(out=lame[:], in_=lams[:], func=AF.Exp), reads=["lams"], writes=["lame"])
        P.add("dve", lambda e: e.scalar_tensor_tensor(out=nlam[:], in0=lame[:, 1:2], scalar=-0.2, in1=lame[:, 0:1], op0=ALU.add, op1=ALU.subtract), reads=["lame"], writes=["nlam"])

        nload = [0]

        def wload(view_fn, src):
            slot = nload[0] % NSLOT
            nload[0] += 1
            dst = view_fn(ring[slot])
            P.add("sp", lambda e: e.dma_start(out=dst, in_=src), reads=SETUP, writes=[("ring", slot)], chan="q_w%d" % slot)
            return slot, dst

        def wload_fm(src_bf, c0, ncols=256):
            return wload(lambda r: r[:, 0:8 * ncols].rearrange("p (kc n) -> p kc n", kc=8),
                         src_bf.rearrange("(kc p) n -> p kc n", p=128)[:, :, c0:c0 + ncols])

        def wload_tm(src_bf, c0, kc0):
            return wload(lambda r: r[:, 0:2048].rearrange("p (kc n) -> p kc n", kc=4),
                         src_bf.rearrange("(kc p) n -> p kc n", p=128)[:, kc0:kc0 + 4, c0:c0 + 512])

        def mm(out, lhsT, rhs, bank, start, stop, reads):
            P.add("pe", lambda e: e.matmul(out, lhsT, rhs, start=start, stop=stop), reads=reads, writes=[PS(bank)])

        def rstd_from(src, dst, n, inv_n, rkeys, wkey, tmp, tkey):
            P.add("act", lambda e: e.activation(out=tmp, in_=src, func=AF.Ln, scale=inv_n, bias=eps_t[:, 0:1]), reads=rkeys + ["eps_t"], writes=[tkey])
            P.add("act", lambda e: e.activation(out=dst, in_=tmp, func=AF.Exp, scale=-0.5), reads=[tkey], writes=[wkey])

        eps_t = sb("eps_t", [128, 1])
        P.add("pool", lambda e: e.memset(eps_t[:], EPS), writes=["eps_t"])

        def norm_T(gT, gkey):
            P.add("pool", lambda e: e.memset(ssq[:], 0.0), writes=["ssq"])
            for s in range(4):
                P.add("act", lambda e, s=s: e.activation(out=junk[:], in_=h[:, s, :], func=AF.Square, accum_out=ssq[:, s:s + 1]),
                      reads=["h%d" % s, "ssq"], writes=["hn_tok1", "ssq"])
            rstd_from(ssq[:], rstd[:], 4, 1.0 / 1024.0, ["ssq"], "rstd", lnv[:], "lnv")
            for s in range(4):
                ht = hn_tok[s % 2]
                hk = "hn_tok%d" % (s % 2)
                P.add("dve", lambda e, s=s, ht=ht: e.tensor_scalar(out=ht[:], in0=h[:, s, :], scalar1=rstd[:, s:s + 1], scalar2=None, op0=ALU.mult),
                      reads=["h%d" % s, "rstd"], writes=[hk])
                tb = 6 + (s % 2)
                tv = trv[s % 2]
                for kc in range(8):
                    P.add("pe", lambda e, kc=kc, ht=ht, tv=tv: e.transpose(tv[:, kc * 128:(kc + 1) * 128], ht[:, kc * 128:(kc + 1) * 128], ident_bf[:]),
                          reads=[hk] + SETUP, writes=[PS(tb)])
                P.add("dve", lambda e, s=s, tv=tv: e.tensor_tensor(
                    out=hnT[:, :, s * 128:(s + 1) * 128],
                    in0=tv[:, :].rearrange("p (kc t) -> p kc t", kc=8),
                    in1=gT[:, :].unsqueeze(2).to_broadcast([128, 8, 128]), op=ALU.mult),
                    reads=[PS(tb)] + SETUP, writes=["hnT"])

        def proj_fm(slot, wv, c, bank, M=128):
            for kc in range(8):
                mm(pbk[bank][0:M, :], wv[:, kc, c:c + M], hnT[:, kc, :], bank, kc == 0, kc == 7, [("ring", slot), "hnT"])

        def proj_tm(slot, wv, c0, ncols, s, bank):
            for kc in range(8):
                mm(pbk[bank][:, 0:ncols], hnT[:, kc, s * 128:(s + 1) * 128], wv[:, kc, c0:c0 + ncols], bank, kc == 0, kc == 7, [("ring", slot), "hnT"])

        def tm_half(slot, wv4, kbase, s, bank, act=None, akeys=None):
            for k in range(4):
                kc = kbase + k
                if act is None:
                    lhs, rk = hnT[:, kc, s * 128:(s + 1) * 128], ["hnT"]
                else:
                    lhs, rk = act[:, kc, s * 128:(s + 1) * 128], [akeys % kc]
                mm(pbk[bank][:, :], lhs, wv4[:, k, :], bank, kc == 0, kc == 7, [("ring", slot)] + rk)

        def head_norm(src_f32, skey, gcol, dst_bf, dkeys, extra_in1, extra_key, extra_scale, nb_bank, f_a, fa_key, b_a_, ba_key):
            P.add("act", lambda e: e.activation(out=b_a_[:], in_=src_f32, func=AF.Square), reads=[skey], writes=[ba_key])
            mm(pbk[nb_bank][:, :], ones_bf[:], b_a_[:], nb_bank, True, True, [ba_key, "ones_bf"])
            P.add("act", lambda e: e.activation(out=f_a[:], in_=pbk[nb_bank][:, :], func=AF.Ln, scale=1.0 / 128.0, bias=eps_t[:, 0:1]), reads=[PS(nb_bank), "eps_t"], writes=[fa_key])
            P.add("act", lambda e: e.activation(out=f_a[:], in_=f_a[:], func=AF.Exp, scale=-0.5), reads=[fa_key], writes=[fa_key])
            if extra_in1 is None:
                P.add("dve", lambda e: e.scalar_tensor_tensor(out=dst_bf, in0=src_f32, scalar=gcol, in1=f_a[:], op0=ALU.mult, op1=ALU.mult),
                      reads=[skey, fa_key] + SETUP + ["dn8"], writes=dkeys)
            else:
                P.add("dve", lambda e: e.scalar_tensor_tensor(out=f_a[:], in0=src_f32, scalar=gcol, in1=f_a[:], op0=ALU.mult, op1=ALU.mult),
                      reads=[skey, fa_key] + SETUP, writes=[fa_key])
                P.add("dve", lambda e: e.scalar_tensor_tensor(out=dst_bf, in0=f_a[:], scalar=extra_scale, in1=extra_in1, op0=ALU.mult, op1=ALU.mult),
                      reads=[fa_key, extra_key], writes=dkeys)

        def phase(name):
            MARKS.append((len(MARKS), name, P.count.get("pe", 0)))

        for g in range(ntiles_total):
            phase("tile%d" % g)
            b = g // NTI
            i = g % NTI
            tok0 = b * SEQ + i * T
            for s in range(4):
                P.add("sp", lambda e, s=s: e.dma_start(out=h[:, s, :], in_=x_d[tok0 + s * 128: tok0 + (s + 1) * 128, :]), writes=["h%d" % s], chan="q_x%d" % s)
            P.add("pool", lambda e: e.dma_start(out=p_bf[:], in_=p_d[tok0:tok0 + T, :].rearrange("(s p) n -> p s n", p=128)), writes=["p_bf"], chan="q_p")
            P.add("pool", lambda e: e.dma_start(out=posf[:], in_=pos_d[:, tok0:tok0 + T].partition_broadcast(128)), writes=["posf"], chan="q_pos")
            if i == 0:
                P.add("pool", lambda e: e.memset(S[:], 0.0), writes=["S0_0", "S1_0", "S0_1", "S1_1"])
                P.add("pool", lambda e: e.memset(Sbf[:, 0, :, :], 0.0), writes=["Sbf0_0", "Sbf1_0"])
                P.add("pool", lambda e: e.memset(HB[:], 0.0), writes=["HB"])
                if g == 0:
                    P.add("pool", lambda e: e.memset(kdecP[:], 0.0), writes=["kdecP0", "kdecP1", "kdecP2", "kdecP3"])

            ang, angc, kf, rr = fs[0], fs[1], fs[2], fs[3]
            ki = fs[4][:, :].bitcast(I32)
            P.add("dve", lambda e: e.tensor_scalar(out=ang[:], in0=posf[:], scalar1=invf[:, 0:1], scalar2=None, op0=ALU.mult), reads=["posf"] + SETUP, writes=["fs0"])
            P.add("dve", lambda e: e.tensor_scalar(out=angc[:], in0=ang[:], scalar1=math.pi / 2, scalar2=None, op0=ALU.add), reads=["fs0"], writes=["fs1"])
            for (src, skey, dst, dkey, scl) in ((ang, "fs0", sinT, "sinT", sgn), (angc, "fs1", cosT, "cosT", None)):
                P.add("dve", lambda e, src=src: e.tensor_scalar(out=ki, in0=src[:], scalar1=1.0 / (2 * math.pi), scalar2=None, op0=ALU.mult), reads=[skey], writes=["fs4"])
                P.add("dve", lambda e: e.tensor_copy(out=kf[:], in_=ki), reads=["fs4"], writes=["fs2"])
                P.add("dve", lambda e, src=src: e.scalar_tensor_tensor(out=rr[:], in0=kf[:], scalar=-2 * math.pi, in1=src[:], op0=ALU.mult, op1=ALU.add), reads=["fs2", skey], writes=["fs3"])
                P.add("dve", lambda e: e.tensor_scalar(out=rr[:], in0=rr[:], scalar1=math.pi, scalar2=-math.pi, op0=ALU.min, op1=ALU.max), reads=["fs3"], writes=["fs3"])
                if scl is not None:
                    P.add("act", lambda e, dst=dst: e.activation(out=dst[:], in_=rr[:], func=AF.Sin, scale=sgn[:, 0:1]), reads=["fs3"] + SETUP, writes=[dkey])
                else:
                    P.add("act", lambda e, dst=dst: e.activation(out=dst[:], in_=rr[:], func=AF.Sin), reads=["fs3"], writes=[dkey])

            phase("norm1")
            norm_T(gT1, "gT1")

            phase("w_in_gla")
            for half in range(2):
                slot, wv4 = wload_tm(win_bf, 512, 4 * half)
                for s in range(4):
                    tm_half(slot, wv4, 4 * half, s, s)
            for s in range(4):
                P.add("act", lambda e: e.copy(out=gv_tok[:, s, :], in_=pbk[s][:, :]), reads=[PS(s)], writes=["gv_tok%d" % s])

            for half in range(2):
                slot, wv = wload_fm(win_bf, 1024 + 256 * half)
                for q in range(2):
                    hh = 2 * half + q
                    bk = hh % 2
                    proj_fm(slot, wv, q * 128, bk)
                    th = fs[4 + bk]
                    P.add("act", lambda e: e.activation(out=th[:], in_=pbk[bk][:, :], func=AF.Tanh, scale=0.5), reads=[PS(bk)], writes=["fs%d" % (4 + bk)])
                    P.add("dve", lambda e: e.scalar_tensor_tensor(out=sg[:, hh, :], in0=th[:], scalar=1.0, in1=pbk[bk][:, :], op0=ALU.add, op1=ALU.mult),
                          reads=["fs%d" % (4 + bk), PS(bk)], writes=["sg%d" % hh])
            for kc in range(8):
                mm(pbk[2][0:16, :], wga[:, kc, :], hnT[:, kc, :], 2, kc == 0, kc == 7, ["hnT"] + SETUP)
            P.add("act", lambda e: e.copy(out=aT[:], in_=pbk[2][0:16, :]), reads=[PS(2)], writes=["aT"])
            for s in range(4):
                bk = 3 + s // 2
                mm(pbk[bk][:, (s % 2) * 256:(s % 2) * 256 + 256], aT[:, s * 128:(s + 1) * 128], wau[:, :], bk, True, True, ["aT"] + SETUP)
            for hf in range(2):
                bk = 3 + hf
                xl = fs[6 + hf][:, :].rearrange("p (s n) -> p s n", s=2)
                xk = "fs%d" % (6 + hf)
                P.add("dve", lambda e, hf=hf, bk=bk: e.tensor_tensor(
                    out=xl, in0=pbk[bk][:, :].rearrange("p (s n) -> p s n", s=2),
                    in1=ba_bc[:, :].unsqueeze(1).to_broadcast([128, 2, 256]), op=ALU.add), reads=[PS(bk)] + SETUP, writes=[xk])
                P.add("act", lambda e, hf=hf: e.activation(out=xl, in_=xl, func=AF.Exp, scale=-1.0), reads=[xk], writes=[xk])
                P.add("act", lambda e, hf=hf: e.activation(out=spv[:, 2 * hf:2 * hf + 2, :], in_=xl, func=AF.Ln, scale=1.0, bias=1.0), reads=[xk], writes=["spv%d" % hf])
            for pr in range(2):
                for s in range(4):
                    mm(pbk[pr][:, s * 128:(s + 1) * 128], spv[:, s, pr * 128:(pr + 1) * 128], mfs[:, :], pr, True, True, ["spv%d" % (s // 2)] + SETUP)
            for s in range(4):
                bk = 3 + s // 2
                mm(pbk[bk][:, (s % 2) * 256:(s % 2) * 256 + 256], mbs[:, :], spv[:, s, :], bk, True, True, ["spv%d" % (s // 2)] + SETUP)
            for pr in range(2):
                P.add("act", lambda e, pr=pr: e.activation(out=eb[:, pr, :], in_=pbk[pr][:, :], func=AF.Exp), reads=[PS(pr)], writes=["eb%d" % pr])
                P.add("act", lambda e, pr=pr: e.activation(out=enb[:, pr, :], in_=pbk[pr][:, :], func=AF.Exp, scale=-1.0), reads=[PS(pr)], writes=["enb%d" % pr])
                P.add("act", lambda e, pr=pr: e.activation(out=dec[:, pr, :], in_=pbk[pr][:, :].rearrange("p (c t) -> p c t", t=64)[:, :, 63], func=AF.Exp), reads=[PS(pr)], writes=["dec"])
            for hf in range(2):
                bk = 3 + hf
                P.add("act", lambda e, hf=hf, bk=bk: e.activation(out=edec[:, 2 * hf:2 * hf + 2, :], in_=pbk[bk][:, :].rearrange("p (s n) -> p s n", s=2), func=AF.Exp), reads=[PS(bk)], writes=["edec%d" % hf])

            slot, wv = wload_fm(win_bf, 0)
            for pr in range(2):
                bk = pr
                proj_fm(slot, wv, pr * 128, bk)
                P.add("dve", lambda e, pr=pr, bk=bk: e.scalar_tensor_tensor(out=qf[:, pr, :], in0=pbk[bk][:, :], scalar=0.125, in1=eb[:, pr, :], op0=ALU.mult, op1=ALU.mult), reads=[PS(bk), "eb%d" % pr], writes=["qf%d" % pr])
                P.add("dve", lambda e, pr=pr, bk=bk: e.scalar_tensor_tensor(out=qe[:, pr, :], in0=pbk[bk][:, :], scalar=0.125, in1=enb[:, pr, :], op0=ALU.mult, op1=ALU.mult), reads=[PS(bk), "enb%d" % pr], writes=["qe%d" % pr])
            slot, wv = wload_fm(win_bf, 256)
            for pr in range(2):
                bk = 3 + pr
                proj_fm(slot, wv, pr * 128, bk)
                P.add("dve", lambda e, pr=pr, bk=bk: e.tensor_tensor(out=ke[:, pr, :], in0=pbk[bk][:, :], in1=enb[:, pr, :], op=ALU.mult), reads=[PS(bk), "enb%d" % pr], writes=["ke%d" % pr])
                P.add("dve", lambda e, pr=pr, bk=bk: e.tensor_tensor(out=kb[:, pr, :], in0=pbk[bk][:, :], in1=eb[:, pr, :], op=ALU.mult), reads=[PS(bk), "eb%d" % pr], writes=["kb%d" % pr])
            for s in range(4):
                bk = 5 + (s % 2)
                proj_tm(slot, wv, 0, 256, s, bk)
                for par in range(2):
                    P.add("dve", lambda e, s=s, bk=bk, par=par: e.tensor_tensor(
                        out=kdecP[:, s, par::2, par * 64:par * 64 + 64],
                        in0=pbk[bk][:, 0:256].rearrange("p (h d) -> p h d", h=4)[:, par::2, :],
                        in1=edec[:, s, :].rearrange("p (h d) -> p h d", h=4)[:, par::2, :], op=ALU.mult),
                        reads=[PS(bk), "edec%d" % (s // 2)], writes=["kdecP%d" % s])

            if dbg and g == 0:
                P.add("pool", lambda e: e.dma_start(out=d_sp, in_=spv), reads=["spv0", "spv1"], chan="q_dbg")
                for n_, (a_, k_) in enumerate(((eb, "eb"), (qf, "qf"), (ke, "ke"), (enb, "enb"))):
                    P.add("pool", lambda e: e.dma_start(out=d_g[:, n_ * 2 * T:(n_ + 1) * 2 * T], in_=a_[:, :, :].rearrange("p a t -> p (a t)")), reads=[k_ + "0", k_ + "1"], chan="q_dbg")
                P.add("pool", lambda e: e.dma_start(out=d_gv, in_=gv_tok[:, :, :].rearrange("p a t -> p (a t)")), reads=["gv_tok%d" % k for k in range(4)], chan="q_dbg")
            phase("gla_core")
            AE, AO, OBE, OBO, DSE, DSO, NB = 0, 1, 2, 3, 4, 5, 6
            ob_store = [(Ub[0][:, 0:T], "Ub0"), (Ub[1][:, 0:T], "Ub1"), (fs[6], "fs6"), (fs[7], "fs7")]
            for pr in range(2):
                for s in range(4):
                    sl = slice(s * 128, (s + 1) * 128)
                    for par in range(2):
                        bk = AE if par == 0 else AO
                        rs = slice(par * 64, par * 64 + 64)
                        rk = ["ke%d" % pr, "qf%d" % pr, "kb%d" % pr, "qe%d" % pr]
                        mm(pbk[bk][:, 0:128], ke[rs, pr, sl], qf[rs, pr, sl], bk, True, True, rk)
                        mm(pbk[bk][:, 128:256], kb[rs, pr, sl], qe[rs, pr, sl], bk, True, True, rk)
                    ab = Abf[s % 2]
                    abk = "Abf%d" % (s % 2)
                    for par in range(2):
                        bk = AE if par == 0 else AO
                        i1, i2 = (s % 2) * 4 + par * 2, (s % 2) * 4 + par * 2 + 1
                        t1, t2 = fs[i1], fs[i2]
                        P.add("dve", lambda e, bk=bk, t1=t1: e.tensor_tensor(out=t1[:, 0:128], in0=pbk[bk][:, 0:128], in1=mf[:, :], op=ALU.mult), reads=[PS(bk)] + SETUP, writes=["fs%d" % i1])
                        P.add("dve", lambda e, bk=bk, t2=t2: e.tensor_tensor(out=t2[:, 0:128], in0=pbk[bk][:, 128:256], in1=mb[:, :], op=ALU.mult), reads=[PS(bk)] + SETUP, writes=["fs%d" % i2])
                        P.add("pool", lambda e, t1=t1, t2=t2, ab=ab, par=par: e.tensor_tensor(out=ab[:, par, :], in0=t1[:, 0:128], in1=t2[:, 0:128], op=ALU.add),
                              reads=["fs%d" % i1, "fs%d" % i2], writes=[abk + "_%d" % par])
                    for par in range(2):
                        hh = 2 * pr + par
                        bk = OBE if par == 0 else OBO
                        mm(pbk[bk][:, sl], gv_tok[:, s, hh * 128:(hh + 1) * 128], ab[:, par, :], bk, True, True, ["gv_tok%d" % s, abk + "_%d" % par])
                    for cc in range(2):
                        bk = DSE if cc == 0 else DSO
                        rs = slice(cc * 64, cc * 64 + 64)
                        for par in range(2):
                            hh = 2 * pr + par
                            mm(pbk[bk][:, par * 128:(par + 1) * 128], kdecP[rs, s, hh, :], gv_tok[rs, s, hh * 128:(hh + 1) * 128], bk, True, True, ["kdecP%d" % s, "gv_tok%d" % s])
                        c = 2 * s + cc
                        for par in range(2):
                            ps_ = slice(par * 64, par * 64 + 64)
                            P.add("dve", lambda e: e.scalar_tensor_tensor(out=S[ps_, (c + 1) % 2, pr, :], in0=S[ps_, c % 2, pr, :], scalar=dec[ps_, pr, c:c + 1], in1=pbk[bk][ps_, par * 128:(par + 1) * 128], op0=ALU.mult, op1=ALU.add),
                                  reads=["S%d_%d" % (pr, c % 2), "dec", PS(bk)], writes=["S%d_%d" % (pr, (c + 1) % 2)])
                        P.add("pool", lambda e, pr=pr, c=c: e.tensor_copy(out=Sbf[:, c + 1, pr, :], in_=S[:, (c + 1) % 2, pr, :]), reads=["S%d_%d" % (pr, (c + 1) % 2)], writes=["Sbf%d_%d" % (pr, c + 1)])
                for c in range(8):
                    for par in range(2):
                        bk = AE if par == 0 else AO
                        rs = slice(par * 64, par * 64 + 64)
                        mm(pbk[bk][:, c * 64:(c + 1) * 64], Sbf[rs, c, pr, :], qf[rs, pr, c * 64:(c + 1) * 64], bk, True, True, ["Sbf%d_%d" % (pr, c), "qf%d" % pr])
                P.add("pool", lambda e, pr=pr: e.tensor_copy(out=Sbf[:, 0, pr, :], in_=Sbf[:, 8, pr, :]), reads=["Sbf%d_8" % pr], writes=["Sbf%d_0" % pr])
                for par in range(2):
                    hh = 2 * pr + par
                    bk = OBE if par == 0 else OBO
                    ob, obk = ob_store[hh]
                    P.add("act", lambda e, bk=bk, ob=ob: e.copy(out=ob[:], in_=pbk[bk][:, :]), reads=[PS(bk)], writes=[obk])
                    bi = AE if par == 0 else AO
                    P.add("dve", lambda e: e.tensor_tensor(out=ob[:], in0=pbk[bi][:, :], in1=ob[:], op=ALU.add), reads=[PS(bi), obk], writes=[obk])

            def gla_epilogue(hh):
                ob, obk = ob_store[hh]
                head_norm(ob[:], obk, gnT[:, hh:hh + 1], oT[:, hh, :], ["oT%d" % hh], sg[:, hh, :], "sg%d" % hh, 0.5, 6,
                          fs[4 + hh % 2], "fs%d" % (4 + hh % 2), bs[4 + hh % 2], "bs%d" % (4 + hh % 2))

            phase("dqdk")
            for which, cbase in enumerate((1552, 2064)):
                for hd in range(4):
                    if hd % 2 == 0:
                        sl_, w_ = wload_fm(win_bf, cbase + 128 * hd)
                    bk = hd % 2
                    pbn = 4 + bk
                    proj_fm(sl_, w_, bk * 128, pbn)
                    zb = bs[2 + bk]
                    zk = "bs%d" % (2 + bk)
                    P.add("act", lambda e: e.copy(out=zb[:], in_=pbk[pbn][:, :]), reads=[PS(pbn)], writes=[zk])
                    rb = 7
                    mm(pbk[rb][:, :], perm_bf[:, :], zb[:], rb, True, True, [zk] + SETUP)
                    t1, t2 = fs[bk * 2], fs[bk * 2 + 1]
                    P.add("dve", lambda e, zb=zb, t1=t1: e.tensor_tensor(out=t1[:], in0=zb[:], in1=cosT[:], op=ALU.mult), reads=[zk, "cosT"], writes=["fs%d" % (bk * 2)])
                    P.add("dve", lambda e, rb=rb, t2=t2: e.tensor_tensor(out=t2[:], in0=pbk[rb][:, :], in1=sinT[:], op=ALU.mult), reads=[PS(rb), "sinT"], writes=["fs%d" % (bk * 2 + 1)])
                    if which == 0:
                        dst, dk_ = qT[:, hd, :], "qT%d" % hd
                    else:
                        dst, dk_ = KT[:, hd, i * T:(i + 1) * T], "KT%d" % hd
                    P.add("pool", lambda e, t1=t1, t2=t2, dst=dst: e.tensor_tensor(out=dst, in0=t1[:], in1=t2[:], op=ALU.add),
                          reads=["fs%d" % (bk * 2), "fs%d" % (bk * 2 + 1)], writes=[dk_])
                    if hd % 2 == 1:
                        gla_epilogue(which * 2 + hd // 2)
            for half in range(2):
                slot, wv4 = wload_tm(win_bf, 2576, 4 * half)
                for s in range(4):
                    tm_half(slot, wv4, 4 * half, s, s)
            for s in range(4):
                P.add("act", lambda e: e.copy(out=V[:, i * 4 + s, :], in_=pbk[s][:, :]), reads=[PS(s)], writes=["V"])

            phase("attn")
            SAb, SBb, OA0, OA1, ZA0, ZA1 = (0, 2), (1, 3), 4, 5, 6, 7
            cnt = 0
            deferred = []

            def attn_deferred():
                while deferred:
                    hd_ = deferred.pop(0)
                    P.add("act", lambda e: e.activation(out=fs[4][:], in_=fs[4][:], func=AF.Exp, scale=-1.0), reads=["fs4"], writes=["fs4"])
                    P.add("act", lambda e: e.activation(out=fs[5][:], in_=fs[5][:], func=AF.Ln), reads=["fs5"], writes=["fs5"])
                    P.add("act", lambda e: e.activation(out=fs[5][:], in_=fs[5][:], func=AF.Exp, scale=-1.0), reads=["fs5"], writes=["fs5"])
                    P.add("dve", lambda e: e.tensor_tensor(out=fs[6][:], in0=fs[1][:], in1=fs[4][:], op=ALU.mult), reads=["fs1", "fs4"], writes=["fs6"])
                    P.add("dve", lambda e: e.tensor_tensor(out=fs[5][:], in0=fs[2][:], in1=fs[5][:], op=ALU.mult), reads=["fs2", "fs5"], writes=["fs5"])
                    P.add("dve", lambda e: e.scalar_tensor_tensor(out=fs[7][:], in0=fs[5][:], scalar=nlam[:, 0:1], in1=fs[6][:], op0=ALU.mult, op1=ALU.add), reads=["fs5", "fs6", "nlam"], writes=["fs7"])
                    head_norm(fs[7][:], "fs7", dn8[:, hd_:hd_ + 1], oT[:, 4 + hd_, :], ["oT%d" % (4 + hd_)], None, None, None, ZA0,
                              fs[4], "fs4", bs[4], "bs4")

            evac_pending = []

            def attn_evac():
                while evac_pending:
                    hd_ = evac_pending.pop(0)
                    P.add("act", lambda e: e.copy(out=fs[1][:], in_=pbk[OA0][:, :]), reads=[PS(OA0)], writes=["fs1"])
                    P.add("dve", lambda e: e.tensor_copy(out=fs[2][:], in_=pbk[OA1][:, :]), reads=[PS(OA1)], writes=["fs2"])
                    P.add("act", lambda e: e.activation(out=fs[4][:], in_=pbk[ZA0][:, :], func=AF.Ln), reads=[PS(ZA0)], writes=["fs4"])
                    P.add("dve", lambda e: e.tensor_copy(out=fs[5][:], in_=pbk[ZA1][:, :]), reads=[PS(ZA1)], writes=["fs5"])
                    deferred.append(hd_)

            for hd in range(4):
                ktiles = [(j, 0) for j in range(4 * i)] + [(4 * i + kt, 128 * kt) for kt in range(4)]
                if i == 0:
                    pass
                nk = len(ktiles)
                pend = None
                pidx = 0 if hd % 2 == 0 else 3
                pacc = (fs[pidx], None)
                pkey = "fs%d" % pidx
                P.add("pool", lambda e: e.memset(pacc[0][:], 0.0), writes=[pkey])
                for n in range(nk + 1):
                    cur = None
                    if n < nk:
                        j, c0 = ktiles[n]
                        pp = cnt % 2
                        cnt += 1
                        cs = slice(c0, T)
                        ks = slice(j * 128, (j + 1) * 128)
                        mm(pbk[SAb[pp]][:, cs], KT[0:64, hd, ks], qT[0:64, hd, cs], SAb[pp], True, True, ["KT%d" % hd, "qT%d" % hd])
                        mm(pbk[SBb[pp]][:, cs], KT[64:128, hd, ks], qT[64:128, hd, cs], SBb[pp], True, True, ["KT%d" % hd, "qT%d" % hd])
                        P0, P1 = bs[2 * pp], bs[2 * pp + 1]
                        k0, k1 = "bs%d" % (2 * pp), "bs%d" % (2 * pp + 1)
                        P.add("act", lambda e: e.activation(out=Pb[pp][:, :, cs], in_=pb2[pp][:, :].rearrange("p (m t) -> p m t", m=2)[:, :, cs], func=AF.Exp, scale=0.125),
                              reads=[PS(SAb[pp]), PS(SBb[pp])], writes=[k0, k1])
                        if j >= 4 * i:
                            P.add("pool", lambda e: e.memset(Pb[pp][64:128, :, c0:c0 + 64], 0.0), writes=[k0, k1])
                        P.add("dve", lambda e: e.tensor_tensor(out=pacc[0][:, cs], in0=pacc[0][:, cs], in1=P0[:, cs], op=ALU.add), reads=[k0, pkey], writes=[pkey])
                        if n == 0:
                            attn_evac()
                        cur = (n, j, cs, P0, P1, k0, k1)
                    if pend is not None:
                        (n_, j_, cs_, Q0, Q1, q0, q1) = pend
                        st, sp_ = (n_ == 0), (n_ == nk - 1)
                        vv = V[:, j_, hd * 128:(hd + 1) * 128]
                        mm(pbk[OA0][:, cs_], vv, Q0[:, cs_], OA0, st, sp_, ["V", q0])
                        mm(pbk[OA1][:, cs_], vv, Q1[:, cs_], OA1, st, sp_, ["V", q1])
                        mm(pbk[ZA1][:, cs_], ones_bf[:, :], Q1[:, cs_], ZA1, st, sp_, ["ones_bf", q1])
                    pend = cur
                    if n == 2:
                        attn_deferred()
                mm(pbk[ZA0][:, :], ones_f[:, :], pacc[0][:], ZA0, True, True, ["ones_f", pkey])
                evac_pending.append(hd)
            attn_evac()
            attn_deferred()

            if dbg and g == 0:
                P.add("pool", lambda e: e.dma_start(out=d_oT, in_=oTf[:, :]), reads=["oT%d" % k for k in range(8)], chan="q_dbg")
                P.add("pool", lambda e: e.dma_start(out=d_qk[:, 0:4 * T], in_=qT[:, :, :].rearrange("p a t -> p (a t)")), reads=["qT%d" % k for k in range(4)], chan="q_dbg")
                P.add("pool", lambda e: e.dma_start(out=d_qk[:, 4 * T:5 * T], in_=KT[:, 0, 0:T]), reads=["KT0"], chan="q_dbg")
            phase("wout")
            for hf in range(2):
                for half in range(2):
                    slot, wv4 = wload_tm(wout_bf, hf * 512, 4 * half)
                    for s in range(4):
                        tm_half(slot, wv4, 4 * half, s, hf * 4 + s, act=oT, akeys="oT%d")
                for s in range(4):
                    bk = hf * 4 + s
                    P.add("dve", lambda e, s=s, hf=hf, bk=bk: e.tensor_tensor(out=h[:, s, hf * 512:(hf + 1) * 512], in0=pbk[bk][:, :], in1=h[:, s, hf * 512:(hf + 1) * 512], op=ALU.add),
                          reads=[PS(bk), "h%d" % s], writes=["h%d" % s])

            if dbg and g == 0:
                P.add("sp", lambda e: e.dma_start(out=d_h, in_=h[:, :, :]), reads=["h0", "h1", "h2", "h3"], chan="q_dbg2")
            phase("ffn_norm")
            norm_T(gT2, "gT2")
            phase("ffn_up")
            SQC = math.sqrt(0.044715)
            blocks = {}

            def ffn_A(j):
                gb, q = j // 2, j % 2
                if q == 0:
                    blocks[gb] = (wload_fm(wup_bf, gb * 256), wload_fm(wup_bf, DFF + gb * 256))
                for which in range(2):
                    sl_, w_ = blocks[gb][which]
                    bk = (j % 2) * 2 + which
                    proj_fm(sl_, w_, q * 128, bk)

            def ffn_B(j):
                for which in range(2):
                    ch = j + 22 * which
                    bk = (j % 2) * 2 + which
                    ub = Ub[which]
                    uk = "Ub%d" % which
                    fi = (j % 3) * 2 + which
                    cbuf, ck = fs[fi], "fs%d" % fi
                    P.add("pool", lambda e: e.tensor_copy(out=ub[:, 0:2], in_=HB[:, ch, :]), reads=["HB%d" % ch, "HB"], writes=[uk])
                    P.add("act", lambda e: e.copy(out=ub[:, 2:T + 2], in_=pbk[bk][:, :]), reads=[PS(bk)], writes=[uk])
                    P.add("act", lambda e: e.activation(out=cbuf[:], in_=pbk[bk][:, :], func=AF.Identity, scale=cwT[:, 2, ch:ch + 1], bias=cbT[:, ch:ch + 1]),
                          reads=[PS(bk), "cwh"] + SETUP, writes=[ck])
                    P.add("dve", lambda e: e.scalar_tensor_tensor(out=cbuf[:], in0=ub[:, 1:T + 1], scalar=cwT[:, 1, ch:ch + 1], in1=cbuf[:], op0=ALU.mult, op1=ALU.add),
                          reads=[uk, ck, "cwh"] + SETUP, writes=[ck])
                    P.add("dve", lambda e: e.scalar_tensor_tensor(out=cbuf[:], in0=ub[:, 0:T], scalar=cwT[:, 0, ch:ch + 1], in1=cbuf[:], op0=ALU.mult, op1=ALU.add),
                          reads=[uk, ck, "cwh"] + SETUP, writes=[ck])
                    P.add("pool", lambda e: e.tensor_copy(out=HB[:, ch, :], in_=ub[:, T:T + 2]), reads=[uk], writes=["HB%d" % ch])

            def ffn_bufs(j):
                base = (j % 3) * 2
                ti = 6 + (j % 2)
                return (fs[base], fs[base + 1], fs[ti], "fs%d" % base, "fs%d" % (base + 1), "fs%d" % ti)

            def ffn_C1(j):
                cg, cv, tmp, kcg, kcv, ktm = ffn_bufs(j)
                P.add("act", lambda e: e.activation(out=tmp[:], in_=cg[:], func=AF.Square, scale=SQC), reads=[kcg], writes=[ktm])
                P.add("dve", lambda e: e.scalar_tensor_tensor(out=tmp[:], in0=tmp[:], scalar=1.0, in1=cg[:], op0=ALU.add, op1=ALU.mult), reads=[ktm, kcg], writes=[ktm])
                P.add("pool", lambda e: e.tensor_tensor(out=cv[:], in0=cg[:], in1=cv[:], op=ALU.mult), reads=[kcg, kcv], writes=[kcv])

            def ffn_C2(j):
                cg, cv, tmp, kcg, kcv, ktm = ffn_bufs(j)
                P.add("act", lambda e: e.activation(out=tmp[:], in_=tmp[:], func=AF.Tanh, scale=0.7978845608028654), reads=[ktm], writes=[ktm])
                P.add("dve", lambda e: e.scalar_tensor_tensor(out=actT[:, j, :], in0=tmp[:], scalar=1.0, in1=cv[:], op0=ALU.add, op1=ALU.mult), reads=[ktm, kcv], writes=["actT%d" % j])

            wdv = wdn_bf.rearrange("(c p) n -> p c n", p=128)
            dblk = {}

            def ffn_D(j):
                if j % 2 == 0:
                    dblk[0] = wload(lambda r: r[:, 0:2048].rearrange("p (c n) -> p c n", c=2), wdv[:, j:j + 2, :])
                slot, wv = dblk[0]
                for s in range(2):
                    for hf in range(2):
                        bk = 4 + s * 2 + hf
                        mm(pbk[bk][:, :], actT[:, j, s * 128:(s + 1) * 128], wv[:, j % 2, hf * 512:(hf + 1) * 512], bk, j == 0, j == 21, [("ring", slot), "actT%d" % j])

            ffn_A(0)
            for t in range(25):
                if t + 1 < 22:
                    ffn_A(t + 1)
                if t < 22:
                    ffn_B(t)
                if 0 <= t - 1 < 22:
                    ffn_C1(t - 1)
                if 0 <= t - 2 < 22:
                    ffn_C2(t - 2)
                if 0 <= t - 3 < 22:
                    ffn_D(t - 3)

            def wd_evac(s, bk, hf):
                P.add("dve", lambda e: e.tensor_tensor(out=h[:, s, hf * 512:(hf + 1) * 512], in0=pbk[bk][:, :], in1=h[:, s, hf * 512:(hf + 1) * 512], op=ALU.add),
                      reads=[PS(bk), "h%d" % s], writes=["h%d" % s])

            phase("w_down")
            for s in range(2):
                for hf in range(2):
                    wd_evac(s, 4 + s * 2 + hf, hf)
            for bl in range(11):
                slot, wv = wload(lambda r: r[:, 0:2048].rearrange("p (c n) -> p c n", c=2), wdv[:, bl * 2:bl * 2 + 2, :])
                for fl in range(2):
                    f = bl * 2 + fl
                    for s in (2, 3):
                        for hf in range(2):
                            bk = (s - 2) * 2 + hf
                            mm(pbk[bk][:, :], actT[:, f, s * 128:(s + 1) * 128], wv[:, fl, hf * 512:(hf + 1) * 512], bk, f == 0, f == 21, [("ring", slot), "actT%d" % f])
            for s in (2, 3):
                for hf in range(2):
                    wd_evac(s, (s - 2) * 2 + hf, hf)

            phase("ple")
            norm_T(gT3, "gT3")
            for s in range(4):
                tb = 6 + (s % 2)
                tv = trv[s % 2]
                for kc in range(2):
                    P.add("pe", lambda e, kc=kc, s=s, tv=tv: e.transpose(tv[:, kc * 128:(kc + 1) * 128], p_bf[:, s, kc * 128:(kc + 1) * 128], ident_bf[:]),
                          reads=["p_bf"] + SETUP, writes=[PS(tb)])
                P.add("dve", lambda e, s=s, tv=tv: e.tensor_copy(out=pT[:, :, s * 128:(s + 1) * 128], in_=tv[:, 0:256].rearrange("p (kc t) -> p kc t", kc=2)), reads=[PS(tb)], writes=["qT0", "qT1"])
            for hf in range(2):
                for half in range(2):
                    slot, wv4 = wload_tm(wpg_bf, hf * 512, 4 * half)
                    for s in range(4):
                        tm_half(slot, wv4, 4 * half, s, s)
                for s in range(4):
                    bg = s
                    bp = 4 + (s % 2)
                    for kc in range(2):
                        mm(pbk[bp][:, :], pT[:, kc, s * 128:(s + 1) * 128], wpp[:, kc, hf * 512:(hf + 1) * 512], bp, kc == 0, kc == 1, ["qT0", "qT1"] + SETUP)
                    th = fs[s]
                    tk = "fs%d" % s
                    P.add("act", lambda e: e.activation(out=th[:], in_=pbk[bg][:, :], func=AF.Tanh, scale=0.5), reads=[PS(bg)], writes=[tk])
                    P.add("dve", lambda e: e.scalar_tensor_tensor(out=th[:], in0=th[:], scalar=1.0, in1=pbk[bp][:, :], op0=ALU.add, op1=ALU.mult), reads=[tk, PS(bp)], writes=[tk])
                    P.add("dve", lambda e: e.scalar_tensor_tensor(out=h[:, s, hf * 512:(hf + 1) * 512], in0=th[:], scalar=0.5, in1=h[:, s, hf * 512:(hf + 1) * 512], op0=ALU.mult, op1=ALU.add),
                          reads=[tk, "h%d" % s], writes=["h%d" % s])

            phase("final")
            P.add("pool", lambda e: e.memset(ssq[:], 0.0), writes=["ssq"])
            for s in range(4):
                P.add("act", lambda e, s=s: e.activation(out=junk[:], in_=h[:, s, :], func=AF.Square, accum_out=ssq[:, s:s + 1]), reads=["h%d" % s, "ssq"], writes=["hn_tok1", "ssq"])
            rstd_from(ssq[:], rstd[:], 4, 1.0 / 1024.0, ["ssq"], "rstd", lnv[:], "lnv")
            for s in range(4):
                P.add("dve", lambda e, s=s: e.scalar_tensor_tensor(out=h[:, s, :], in0=h[:, s, :], scalar=rstd[:, s:s + 1], in1=gf_bc[:, :], op0=ALU.mult, op1=ALU.mult),
                      reads=["h%d" % s, "rstd"] + SETUP, writes=["h%d" % s])
                P.add("sp", lambda e, s=s: e.dma_start(out=y_d[tok0 + s * 128: tok0 + (s + 1) * 128, :], in_=h[:, s, :]), reads=["h%d" % s], chan="q_y%d" % s)

        P.wait_all("sp", ["q_y0", "q_y1", "q_y2", "q_y3", "q_dbg", "q_dbg2"])
        P.emit()
    return nc


_W_KEYS = ["w_in", "w_out", "w_up", "w_down", "w_ple_gate", "w_ple_proj", "w_a_up"]
_V_KEYS = ["b_a", "norm_mix", "norm_ffn", "norm_ple", "gla_norm", "diff_norm", "conv_b",
           "lam_q1", "lam_k1", "lam_q2", "lam_k2"]


def kernel(**inputs):
    x = np.asarray(inputs["x"], dtype=np.float32)
    p = np.asarray(inputs["p"], dtype=np.float32)[0]
    pos = np.asarray(inputs["positions"]).astype(np.int32)
    shared = {}
    for k in _W_KEYS:
        shared[k] = np.ascontiguousarray(np.asarray(inputs[k], dtype=np.float32)[0])
    for k in _V_KEYS:
        shared[k] = np.ascontiguousarray(np.asarray(inputs[k], dtype=np.float32).reshape(1, -1))
    shared["conv_w"] = np.ascontiguousarray(np.asarray(inputs["conv_w"], dtype=np.float32)[0])
    shared["norm_final"] = np.ascontiguousarray(np.asarray(inputs["norm_final"], dtype=np.float32).reshape(1, -1))
    shared.update(host_consts())
    in_maps = []
    for c in range(NCORES):
        m = dict(shared)
        m["x"] = np.ascontiguousarray(x[c * NSEQ:(c + 1) * NSEQ].reshape(NSEQ * SEQ, 1024))
        m["p"] = np.ascontiguousarray(p[c * NSEQ:(c + 1) * NSEQ].reshape(NSEQ * SEQ, 256))
        m["pos"] = np.ascontiguousarray(pos[c * NSEQ:(c + 1) * NSEQ].reshape(1, NSEQ * SEQ))
        in_maps.append(m)
    nc = build_program()
    res = run_bass_kernel_spmd(nc, in_maps, core_ids=list(range(NCORES)))
    out = np.concatenate([np.asarray(r["y"]).reshape(NSEQ, SEQ, 1024) for r in res.results], axis=0)
    return out.astype(np.float32)
```

```python
import bisect
import contextlib
import math
import numpy as np
import concourse.bass as bass
import concourse.mybir as mybir
from concourse.bass_utils import run_bass_kernel_spmd

F32 = mybir.dt.float32
BF16 = mybir.dt.bfloat16
I32 = mybir.dt.int32
AF = mybir.ActivationFunctionType
ALU = mybir.AluOpType

COMPUTE = ("pe", "act", "dve", "pool")
MARKS = []
NCORES = 8
SEQ = 4096
NSEQ = 2
T = 512
NTI = SEQ // T
EPS = 1e-6
DFF = 2816
NSLOT = 4


class _Rec:
    def __getattr__(self, name):
        def f(*a, **k):
            self.call = (name, a, k)
            return self
        return f


class Prog:
    def __init__(self, nc):
        self.nc = nc
        self.streams = {s: [] for s in ("pe", "act", "dve", "pool", "sp")}
        self.count = {}
        self.res = {}
        self.seen = {s: {} for s in self.streams}
        self.need = {}
        self.dma_chans = set()

    def _st(self, key):
        st = self.res.get(key)
        if st is None:
            st = self.res[key] = {"w": None, "r": []}
        return st

    def add(self, stream, fn, reads=(), writes=(), chan=None):
        rec = _Rec()
        fn(rec)
        call = rec.call
        fn = lambda e, call=call: getattr(e, call[0])(*call[1], **call[2])
        if chan is None:
            chan = stream
        else:
            self.dma_chans.add(chan)
        idx = self.count.get(chan, 0) + 1
        self.count[chan] = idx
        me = (chan, idx)
        deps = set()
        for key in reads:
            st = self._st(key)
            if st["w"] is not None:
                deps.add(st["w"])
            if isinstance(key, tuple) and key[0] == "ps":
                for r in st["r"]:
                    if r[0] != chan:
                        deps.add(r)
        for key in writes:
            st = self._st(key)
            if st["w"] is not None:
                deps.add(st["w"])
            deps.update(st["r"])
        for key in reads:
            self._st(key)["r"].append(me)
        for key in writes:
            st = self._st(key)
            st["w"] = me
            st["r"] = []
        waits = []
        seen = self.seen[stream]
        best = {}
        for (c, n) in deps:
            if c == chan:
                if chan == "pe":
                    continue
                if chan in COMPUTE and idx - n > 1:
                    continue
            if n > best.get(c, 0):
                best[c] = n
        for c in sorted(best):
            n = best[c]
            if seen.get(c, 0) >= n:
                continue
            seen[c] = n
            waits.append((c, n))
            self.need.setdefault(c, set()).add(n)
        self.streams[stream].append((fn, waits, chan, idx))
        return me

    def wait_all(self, stream, chans):
        w = []
        for c in chans:
            n = self.count.get(c, 0)
            if n == 0:
                continue
            w.append((c, n))
            self.need.setdefault(c, set()).add(n)
            self.seen[stream][c] = max(self.seen[stream].get(c, 0), n)
        self.streams[stream].append((None, w, None, None))

    def emit(self):
        nc = self.nc
        chans = sorted(set(self.count.keys()))
        sems = {}
        with contextlib.ExitStack() as es:
            for c in chans:
                sems[c] = es.enter_context(nc.semaphore("s_" + c))
            needl = {c: sorted(v) for c, v in self.need.items()}

            def val(c, n):
                if c in self.dma_chans:
                    return 16 * n
                return bisect.bisect_right(needl[c], n)

            def run(stream, eng):
                for (fn, waits, chan, idx) in self.streams[stream]:
                    for (c, n) in waits:
                        v = val(c, n)
                        if v > 0:
                            eng.wait_ge(sems[c], v)
                    if fn is None:
                        continue
                    ins = fn(eng)
                    if chan in self.dma_chans:
                        ins.then_inc(sems[chan], 16)
                    elif idx in self.need.get(chan, ()):
                        ins.then_inc(sems[chan], 1)

            with nc.Block() as block:
                @block.sync
                def _(e):
                    run("sp", e)

                @block.scalar
                def _(e):
                    run("act", e)

                @block.tensor
                def _(e):
                    run("pe", e)

                @block.vector
                def _(e):
                    run("dve", e)

                @block.gpsimd
                def _(e):
                    run("pool", e)


def host_consts():
    s = np.arange(128)[:, None]
    t = np.arange(128)[None, :]
    same = (s // 64) == (t // 64)
    mf = (same & (s <= t)).astype(np.float32)
    mb = (same & (s > t)).astype(np.float32)
    pidx = np.arange(128)
    perm = np.zeros((128, 128), np.float32)
    src = np.where((pidx % 64) < 32, pidx + 32, pidx - 32)
    perm[src, pidx] = 1.0
    j = (pidx % 32).astype(np.float32)
    invf = (np.float32(10000.0) ** (-(2.0 * j) / np.float32(64.0))).astype(np.float32)
    sgn = np.where((pidx % 64) < 32, -1.0, 1.0).astype(np.float32)
    return {
        "c_ident": np.eye(128, dtype=np.float32),
        "c_perm": perm,
        "c_mf": mf, "c_mb": mb,
        "c_mfs": (-mf / 16.0).astype(np.float32), "c_mbs": (-mb / 16.0).astype(np.float32),
        "c_invf": invf.reshape(128, 1), "c_sgn": sgn.reshape(128, 1),
    }


def build_program(ntiles_total=NSEQ * NTI, dbg=False):
    nc = bass.Bass("TRN2", target_bir_lowering=False)

    def din(name, shape, dt=F32):
        return nc.dram_tensor(name, shape, dt, kind="ExternalInput").ap()

    NTOK = NSEQ * SEQ
    x_d = din("x", [NTOK, 1024])
    p_d = din("p", [NTOK, 256])
    pos_d = din("pos", [1, NTOK], I32)
    w_in_d = din("w_in", [1024, 3088])
    w_out_d = din("w_out", [1024, 1024])
    w_up_d = din("w_up", [1024, 5632])
    w_dn_d = din("w_down", [DFF, 1024])
    w_pg_d = din("w_ple_gate", [1024, 1024])
    w_pp_d = din("w_ple_proj", [256, 1024])
    w_au_d = din("w_a_up", [16, 256])
    b_a_d = din("b_a", [1, 256])
    n_mix_d = din("norm_mix", [1, 1024])
    n_ffn_d = din("norm_ffn", [1, 1024])
    n_ple_d = din("norm_ple", [1, 1024])
    n_fin_d = din("norm_final", [1, 1024])
    g_norm_d = din("gla_norm", [1, 512])
    d_norm_d = din("diff_norm", [1, 512])
    cw_d = din("conv_w", [3, 5632])
    cb_d = din("conv_b", [1, 5632])
    lam_d = [din(n, [1, 64]) for n in ("lam_q1", "lam_k1", "lam_q2", "lam_k2")]
    c_ident_d = din("c_ident", [128, 128])
    c_perm_d = din("c_perm", [128, 128])
    c_mf_d = din("c_mf", [128, 128])
    c_mb_d = din("c_mb", [128, 128])
    c_mfs_d = din("c_mfs", [128, 128])
    c_mbs_d = din("c_mbs", [128, 128])
    c_invf_d = din("c_invf", [128, 1])
    c_sgn_d = din("c_sgn", [128, 1])
    y_d = nc.dram_tensor("y", [NTOK, 1024], F32, kind="ExternalOutput").ap()

    win_bf = nc.dram_tensor("win_bf", [1024, 3088], BF16, kind="Internal").ap()
    wout_bf = nc.dram_tensor("wout_bf", [1024, 1024], BF16, kind="Internal").ap()
    wup_bf = nc.dram_tensor("wup_bf", [1024, 5632], BF16, kind="Internal").ap()
    wdn_bf = nc.dram_tensor("wdn_bf", [DFF, 1024], BF16, kind="Internal").ap()
    wpg_bf = nc.dram_tensor("wpg_bf", [1024, 1024], BF16, kind="Internal").ap()

    if dbg:
        d_oT = nc.dram_tensor("d_oT", [128, 8 * T], F32, kind="ExternalOutput").ap()
        d_h = nc.dram_tensor("d_h", [128, 4, 1024], F32, kind="ExternalOutput").ap()
        d_qk = nc.dram_tensor("d_qk", [128, 5 * T], F32, kind="ExternalOutput").ap()
        d_sp = nc.dram_tensor("d_sp", [128, 4, 256], F32, kind="ExternalOutput").ap()
        d_g = nc.dram_tensor("d_g", [128, 8 * T], F32, kind="ExternalOutput").ap()
        d_gv = nc.dram_tensor("d_gv", [128, 4 * 512], F32, kind="ExternalOutput").ap()
    P = Prog(nc)
    with contextlib.ExitStack() as es:
        def sb(name, shape, dt=F32):
            return es.enter_context(nc.sbuf_tensor(name, shape, dt))

        pb2 = [es.enter_context(nc.psum_tensor("pb%d" % i, [128, 1024], F32)) for i in range(4)]
        pbk = []
        for i in range(4):
            pbk.append(pb2[i][:, 0:512])
            pbk.append(pb2[i][:, 512:1024])

        def PS(b):
            return ("ps", b)

        h = sb("h", [128, 4, 1024])
        KT = sb("KT", [128, 4, SEQ], BF16)
        V = sb("V", [128, SEQ // 128, 512], BF16)
        actF = sb("actT", [128, 22 * T], BF16)
        actT = actF[:, :].rearrange("p (j t) -> p j t", j=22)
        hnT = sb("hnT", [128, 8, T], BF16)
        oTf = sb("oT", [128, 8 * T], BF16)
        oT = oTf[:, :].rearrange("p (k t) -> p k t", k=8)
        qT = sb("qT", [128, 4, T], BF16)
        ring = [sb("ring%d" % i, [128, 2048], BF16) for i in range(NSLOT)]
        wga = sb("wga", [128, 8, 16], BF16)
        wpp = sb("wpp", [128, 2, 1024], BF16)
        wau = sb("wau", [16, 256], BF16)
        HB = sb("HB", [128, 44, 2])
        gf_bc = sb("gf_bc", [128, 1024])
        ba_bc = sb("ba_bc", [128, 256])
        gT1 = sb("gT1", [128, 8]); gT2 = sb("gT2", [128, 8]); gT3 = sb("gT3", [128, 8])
        gnT = sb("gnT", [128, 4]); dnT = sb("dnT", [128, 4]); dn8 = sb("dn8", [128, 4])
        cwT = sb("cwT", [128, 3, 44]); cbT = sb("cbT", [128, 44])
        lamv = [sb("lamv%d" % i, [128, 64]) for i in range(4)]
        lamt = sb("lamt", [128, 64]); lams = sb("lams", [128, 2]); lame = sb("lame", [128, 2]); nlam = sb("nlam", [128, 1])
        ident_bf = sb("ident_bf", [128, 128], BF16)
        perm_bf = sb("perm_bf", [128, 128], BF16)
        mf = sb("mf", [128, 128]); mb = sb("mb", [128, 128]); mfs = sb("mfs", [128, 128]); mbs = sb("mbs", [128, 128])
        ones_bf = sb("ones_bf", [128, 128], BF16)
        ones_f = sb("ones_f", [128, 128])
        invf = sb("invf", [128, 1]); sgn = sb("sgn", [128, 1])
        marker = sb("marker", [128, 1])
        ssq = sb("ssq", [128, 4]); lnv = sb("lnv", [128, 4]); rstd = sb("rstd", [128, 4])
        hn_tok = [sb("hn_tok0", [128, 1024], BF16), sb("hn_tok1", [128, 1024], BF16)]
        junk = hn_tok[1]
        posf = sb("posf", [128, T])
        cosT = oTf[:, 4 * T:6 * T].bitcast(F32)
        sinT = oTf[:, 6 * T:8 * T].bitcast(F32)
        p_bf = sb("p_bf", [128, 4, 256], BF16)
        pT = qT[:, 0:2, :]
        sg = sb("sg", [128, 4, T], BF16)
        aT = sb("aT", [16, T], BF16)
        enb = sb("enb", [128, 2, T], BF16)
        dec = sb("dec", [128, 2, 8])
        def av(a, n, shp):
            return actF[:, a:a + n].rearrange("p (j t) -> p j t", j=shp)
        qf = av(0, 1024, 2); qe = av(1024, 1024, 2); ke = av(2048, 1024, 2); kb = av(3072, 1024, 2)
        gv_tok = av(4096, 2048, 4)
        eb = av(10240, 1024, 2)
        spv = actF[:, 6144:8192].bitcast(F32).rearrange("p (s n) -> p s n", s=4)
        edec = actF[:, 8192:10240].bitcast(F32).rearrange("p (s n) -> p s n", s=4)
        kdecP = sb("kdecP", [128, 4, 4, 128], BF16)
        Abf = [sb("Abf%d" % i, [128, 4, 128], BF16) for i in range(2)]
        S = sb("S", [128, 2, 2, 128]); Sbf = sb("Sbf", [128, 9, 2, 128], BF16)
        fs = [sb("fs%d" % i, [128, T]) for i in range(8)]
        Pb = [sb("Pb%d" % i, [128, 2, T], BF16) for i in range(2)]
        bs = [Pb[0][:, 0, :], Pb[0][:, 1, :], Pb[1][:, 0, :], Pb[1][:, 1, :], sb("bs4", [128, T], BF16), sb("bs5", [128, T], BF16)]
        Ub = [sb("Ub%d" % i, [128, T + 2]) for i in range(2)]

        trv = [pbk[6][:].bitcast(BF16), pbk[7][:].bitcast(BF16)]

        ccount = [0]

        def cload(dst, src, key, **kw):
            ccount[0] += 1
            P.add("sp", lambda e: e.dma_start(out=dst, in_=src, **kw), writes=["c%d" % ccount[0]], chan="q_const")

        cload(mf[:], c_mf_d, "c"); cload(mb[:], c_mb_d, "c"); cload(mfs[:], c_mfs_d, "c"); cload(mbs[:], c_mbs_d, "c")
        cload(invf[:], c_invf_d, "c"); cload(sgn[:], c_sgn_d, "c")
        cload(gf_bc[:], n_fin_d.partition_broadcast(128), "c")
        cload(ba_bc[:], b_a_d.partition_broadcast(128), "c")
        for gt, d in ((gT1, n_mix_d), (gT2, n_ffn_d), (gT3, n_ple_d)):
            cload(gt[:], d.rearrange("o (kc p) -> p (o kc)", p=128), "c", allow_slow_non_contiguous=True)
        cload(gnT[:], g_norm_d.rearrange("o (h p) -> p (o h)", p=128), "c", allow_slow_non_contiguous=True)
        cload(dnT[:], d_norm_d.rearrange("o (h p) -> p (o h)", p=128), "c", allow_slow_non_contiguous=True)
        cload(cwT[:], cw_d.rearrange("j (c p) -> p j c", p=128), "c", allow_slow_non_contiguous=True)
        cload(cbT[:], cb_d.rearrange("o (c p) -> p (o c)", p=128), "c", allow_slow_non_contiguous=True)
        for i in range(4):
            cload(lamv[i][:], lam_d[i].partition_broadcast(128), "c")

        def castdma(dst, src):
            P.add("pool", lambda e: e.dma_start(out=dst, in_=src), chan="q_cast")

        for k in range(8):
            r0, r1 = k * 128, (k + 1) * 128
            castdma(win_bf[r0:r1, :], w_in_d[r0:r1, :])

        def cast_stage2():
            for k in range(8):
                r0, r1 = k * 128, (k + 1) * 128
                castdma(wout_bf[r0:r1, :], w_out_d[r0:r1, :])
                castdma(wup_bf[r0:r1, :], w_up_d[r0:r1, :])
            for k in range(22):
                r0, r1 = k * 128, (k + 1) * 128
                castdma(wdn_bf[r0:r1, :], w_dn_d[r0:r1, :])
            for k in range(8):
                r0, r1 = k * 128, (k + 1) * 128
                castdma(wpg_bf[r0:r1, :], w_pg_d[r0:r1, :])

        castdma(wpp[:], w_pp_d.rearrange("(kc p) n -> p kc n", p=128))
        castdma(wau[:], w_au_d)
        castdma(wga[:], w_in_d.rearrange("(kc p) n -> p kc n", p=128)[:, :, 1536:1552])
        castdma(ident_bf[:], c_ident_d)
        castdma(perm_bf[:], c_perm_d)
        P.wait_all("pool", ["q_cast", "q_const"])
        P.add("pool", lambda e: e.memset(marker[:], 0.0), writes=["setup"])
        SETUP = ["setup"]
        P.add("pool", lambda e: e.memset(ones_bf[:], 1.0), writes=["ones_bf"])
        P.add("pool", lambda e: e.memset(ones_f[:], 1.0), writes=["ones_f"])
        P.add("dve", lambda e: e.tensor_scalar(out=dn8[:], in0=dnT[:], scalar1=0.8, scalar2=None, op0=ALU.mult), reads=SETUP, writes=["dn8"])
        P.add("dve", lambda e: e.tensor_scalar(out=cwT[:, :, 22:44], in0=cwT[:, :, 22:44], scalar1=0.5, scalar2=None, op0=ALU.mult), reads=SETUP, writes=["cwh"])
        P.add("dve", lambda e: e.tensor_scalar(out=cbT[:, 22:44], in0=cbT[:, 22:44], scalar1=0.5, scalar2=None, op0=ALU.mult), reads=SETUP, writes=["cwh"])
        for i in range(2):
            P.add("dve", lambda e, i=i: e.tensor_tensor(out=lamt[:], in0=lamv[2 * i][:], in1=lamv[2 * i + 1][:], op=ALU.mult), reads=SETUP, writes=["lamt"])
            P.add("dve", lambda e, i=i: e.tensor_reduce(out=lams[:, i:i + 1], in_=lamt[:], axis=mybir.AxisListType.X, op=ALU.add), reads=["lamt"], writes=["lams"])
        P.add("act", lambda e: e.activation(out=lame[:], in_=lams[:], func=AF.Exp), reads=["lams"], writes=["lame"])
        P.add("dve", lambda e: e.scalar_tensor_tensor(out=nlam[:], in0=lame[:, 1:2], scalar=-0.2, in1=lame[:, 0:1], op0=ALU.add, op1=ALU.subtract), reads=["lame"], writes=["nlam"])

        nload = [0]

        def wload(view_fn, src, dep="setup2"):
            slot = nload[0] % NSLOT
            nload[0] += 1
            dst = view_fn(ring[slot])
            P.add("sp", lambda e: e.dma_start(out=dst, in_=src), reads=[dep], writes=[("ring", slot)], chan="q_w%d" % slot)
            return slot, dst

        def wload_fm(src_bf, c0, ncols=256):
            return wload(lambda r: r[:, 0:8 * ncols].rearrange("p (kc n) -> p kc n", kc=8),
                         src_bf.rearrange("(kc p) n -> p kc n", p=128)[:, :, c0:c0 + ncols],
                         dep="setup" if src_bf is win_bf else "setup2")

        def wload_tm(src_bf, c0, kc0):
            return wload(lambda r: r[:, 0:2048].rearrange("p (kc n) -> p kc n", kc=4),
                         src_bf.rearrange("(kc p) n -> p kc n", p=128)[:, kc0:kc0 + 4, c0:c0 + 512],
                         dep="setup" if src_bf is win_bf else "setup2")

        def mm(out, lhsT, rhs, bank, start, stop, reads):
            P.add("pe", lambda e: e.matmul(out, lhsT, rhs, start=start, stop=stop), reads=reads, writes=[PS(bank)])

        def rstd_from(src, dst, n, inv_n, rkeys, wkey, tmp, tkey):
            P.add("act", lambda e: e.activation(out=tmp, in_=src, func=AF.Ln, scale=inv_n, bias=eps_t[:, 0:1]), reads=rkeys + ["eps_t"], writes=[tkey])
            P.add("act", lambda e: e.activation(out=dst, in_=tmp, func=AF.Exp, scale=-0.5), reads=[tkey], writes=[wkey])

        eps_t = sb("eps_t", [128, 1])
        P.add("pool", lambda e: e.memset(eps_t[:], EPS), writes=["eps_t"])

        def norm_T(gT, gkey):
            P.add("pool", lambda e: e.memset(ssq[:], 0.0), writes=["ssq"])
            for s in range(4):
                P.add("act", lambda e, s=s: e.activation(out=junk[:], in_=h[:, s, :], func=AF.Square, accum_out=ssq[:, s:s + 1]),
                      reads=["h%d" % s, "ssq"], writes=["hn_tok1", "ssq"])
            rstd_from(ssq[:], rstd[:], 4, 1.0 / 1024.0, ["ssq"], "rstd", lnv[:], "lnv")
            for s in range(4):
                ht = hn_tok[s % 2]
                hk = "hn_tok%d" % (s % 2)
                P.add("dve", lambda e, s=s, ht=ht: e.tensor_scalar(out=ht[:], in0=h[:, s, :], scalar1=rstd[:, s:s + 1], scalar2=None, op0=ALU.mult),
                      reads=["h%d" % s, "rstd"], writes=[hk])
                tb = 6 + (s % 2)
                tv = trv[s % 2]
                for kc in range(8):
                    P.add("pe", lambda e, kc=kc, ht=ht, tv=tv: e.transpose(tv[:, kc * 128:(kc + 1) * 128], ht[:, kc * 128:(kc + 1) * 128], ident_bf[:]),
                          reads=[hk] + SETUP, writes=[PS(tb)])
                P.add("dve", lambda e, s=s, tv=tv: e.tensor_tensor(
                    out=hnT[:, :, s * 128:(s + 1) * 128],
                    in0=tv[:, :].rearrange("p (kc t) -> p kc t", kc=8),
                    in1=gT[:, :].unsqueeze(2).to_broadcast([128, 8, 128]), op=ALU.mult),
                    reads=[PS(tb)] + SETUP, writes=["hnT"])

        def proj_fm(slot, wv, c, bank, M=128):
            for kc in range(8):
                mm(pbk[bank][0:M, :], wv[:, kc, c:c + M], hnT[:, kc, :], bank, kc == 0, kc == 7, [("ring", slot), "hnT"])

        def proj_tm(slot, wv, c0, ncols, s, bank):
            for kc in range(8):
                mm(pbk[bank][:, 0:ncols], hnT[:, kc, s * 128:(s + 1) * 128], wv[:, kc, c0:c0 + ncols], bank, kc == 0, kc == 7, [("ring", slot), "hnT"])

        def tm_half(slot, wv4, kbase, s, bank, act=None, akeys=None):
            for k in range(4):
                kc = kbase + k
                if act is None:
                    lhs, rk = hnT[:, kc, s * 128:(s + 1) * 128], ["hnT"]
                else:
                    lhs, rk = act[:, kc, s * 128:(s + 1) * 128], [akeys % kc]
                mm(pbk[bank][:, :], lhs, wv4[:, k, :], bank, kc == 0, kc == 7, [("ring", slot)] + rk)

        def head_norm(src_f32, skey, gcol, dst_bf, dkeys, extra_in1, extra_key, extra_scale, nb_bank, f_a, fa_key, b_a_, ba_key):
            P.add("act", lambda e: e.activation(out=b_a_[:], in_=src_f32, func=AF.Square), reads=[skey], writes=[ba_key])
            mm(pbk[nb_bank][:, :], ones_bf[:], b_a_[:], nb_bank, True, True, [ba_key, "ones_bf"])
            P.add("act", lambda e: e.activation(out=f_a[:], in_=pbk[nb_bank][:, :], func=AF.Ln, scale=1.0 / 128.0, bias=eps_t[:, 0:1]), reads=[PS(nb_bank), "eps_t"], writes=[fa_key])
            P.add("act", lambda e: e.activation(out=f_a[:], in_=f_a[:], func=AF.Exp, scale=-0.5), reads=[fa_key], writes=[fa_key])
            if extra_in1 is None:
                P.add("dve", lambda e: e.scalar_tensor_tensor(out=dst_bf, in0=src_f32, scalar=gcol, in1=f_a[:], op0=ALU.mult, op1=ALU.mult),
                      reads=[skey, fa_key] + SETUP + ["dn8"], writes=dkeys)
            else:
                P.add("dve", lambda e: e.scalar_tensor_tensor(out=f_a[:], in0=src_f32, scalar=gcol, in1=f_a[:], op0=ALU.mult, op1=ALU.mult),
                      reads=[skey, fa_key] + SETUP, writes=[fa_key])
                P.add("dve", lambda e: e.scalar_tensor_tensor(out=dst_bf, in0=f_a[:], scalar=extra_scale, in1=extra_in1, op0=ALU.mult, op1=ALU.mult),
                      reads=[fa_key, extra_key], writes=dkeys)

        def phase(name):
            MARKS.append((len(MARKS), name, P.count.get("pe", 0)))

        for g in range(ntiles_total):
            phase("tile%d" % g)
            b = g // NTI
            i = g % NTI
            tok0 = b * SEQ + i * T
            for s in range(4):
                P.add("sp", lambda e, s=s: e.dma_start(out=h[:, s, :], in_=x_d[tok0 + s * 128: tok0 + (s + 1) * 128, :]), writes=["h%d" % s], chan="q_x%d" % s)
            P.add("pool", lambda e: e.dma_start(out=p_bf[:], in_=p_d[tok0:tok0 + T, :].rearrange("(s p) n -> p s n", p=128)), writes=["p_bf"], chan="q_p")
            P.add("pool", lambda e: e.dma_start(out=posf[:], in_=pos_d[:, tok0:tok0 + T].partition_broadcast(128)), writes=["posf"], chan="q_pos")
            if g == 0:
                cast_stage2()
            if i == 0:
                P.add("pool", lambda e: e.memset(S[:], 0.0), writes=["S0_0", "S1_0", "S0_1", "S1_1"])
                P.add("pool", lambda e: e.memset(Sbf[:, 0, :, :], 0.0), writes=["Sbf0_0", "Sbf1_0"])
                P.add("pool", lambda e: e.memset(HB[:], 0.0), writes=["HB"])
                if g == 0:
                    P.add("pool", lambda e: e.memset(kdecP[:], 0.0), writes=["kdecP0", "kdecP1", "kdecP2", "kdecP3"])

            ang, angc, kf, rr = fs[0], fs[1], fs[2], fs[3]
            ki = fs[4][:, :].bitcast(I32)
            P.add("dve", lambda e: e.tensor_scalar(out=ang[:], in0=posf[:], scalar1=invf[:, 0:1], scalar2=None, op0=ALU.mult), reads=["posf"] + SETUP, writes=["fs0"])
            P.add("dve", lambda e: e.tensor_scalar(out=angc[:], in0=ang[:], scalar1=math.pi / 2, scalar2=None, op0=ALU.add), reads=["fs0"], writes=["fs1"])
            for (src, skey, dst, dkey, scl) in ((ang, "fs0", sinT, "sinT", sgn), (angc, "fs1", cosT, "cosT", None)):
                P.add("dve", lambda e, src=src: e.tensor_scalar(out=ki, in0=src[:], scalar1=1.0 / (2 * math.pi), scalar2=None, op0=ALU.mult), reads=[skey], writes=["fs4"])
                P.add("dve", lambda e: e.tensor_copy(out=kf[:], in_=ki), reads=["fs4"], writes=["fs2"])
                P.add("dve", lambda e, src=src: e.scalar_tensor_tensor(out=rr[:], in0=kf[:], scalar=-2 * math.pi, in1=src[:], op0=ALU.mult, op1=ALU.add), reads=["fs2", skey], writes=["fs3"])
                P.add("dve", lambda e: e.tensor_scalar(out=rr[:], in0=rr[:], scalar1=math.pi, scalar2=-math.pi, op0=ALU.min, op1=ALU.max), reads=["fs3"], writes=["fs3"])
                if scl is not None:
                    P.add("act", lambda e, dst=dst: e.activation(out=dst[:], in_=rr[:], func=AF.Sin, scale=sgn[:, 0:1]), reads=["fs3"] + SETUP, writes=[dkey])
                else:
                    P.add("act", lambda e, dst=dst: e.activation(out=dst[:], in_=rr[:], func=AF.Sin), reads=["fs3"], writes=[dkey])

            phase("norm1")
            norm_T(gT1, "gT1")

            phase("w_in_gla")
            for half in range(2):
                slot, wv4 = wload_tm(win_bf, 512, 4 * half)
                for s in range(4):
                    tm_half(slot, wv4, 4 * half, s, s)
            for s in range(4):
                P.add("act", lambda e: e.copy(out=gv_tok[:, s, :], in_=pbk[s][:, :]), reads=[PS(s)], writes=["gv_tok%d" % s])

            for half in range(2):
                slot, wv = wload_fm(win_bf, 1024 + 256 * half)
                for q in range(2):
                    hh = 2 * half + q
                    bk = hh % 2
                    proj_fm(slot, wv, q * 128, bk)
                    th = fs[4 + bk]
                    P.add("act", lambda e: e.activation(out=th[:], in_=pbk[bk][:, :], func=AF.Tanh, scale=0.5), reads=[PS(bk)], writes=["fs%d" % (4 + bk)])
                    P.add("dve", lambda e: e.scalar_tensor_tensor(out=sg[:, hh, :], in0=th[:], scalar=1.0, in1=pbk[bk][:, :], op0=ALU.add, op1=ALU.mult),
                          reads=["fs%d" % (4 + bk), PS(bk)], writes=["sg%d" % hh])
            for kc in range(8):
                mm(pbk[2][0:16, :], wga[:, kc, :], hnT[:, kc, :], 2, kc == 0, kc == 7, ["hnT"] + SETUP)
            P.add("act", lambda e: e.copy(out=aT[:], in_=pbk[2][0:16, :]), reads=[PS(2)], writes=["aT"])
            for s in range(4):
                bk = 3 + s // 2
                mm(pbk[bk][:, (s % 2) * 256:(s % 2) * 256 + 256], aT[:, s * 128:(s + 1) * 128], wau[:, :], bk, True, True, ["aT"] + SETUP)
            for hf in range(2):
                bk = 3 + hf
                xl = fs[6 + hf][:, :].rearrange("p (s n) -> p s n", s=2)
                xk = "fs%d" % (6 + hf)
                P.add("dve", lambda e, hf=hf, bk=bk: e.tensor_tensor(
                    out=xl, in0=pbk[bk][:, :].rearrange("p (s n) -> p s n", s=2),
                    in1=ba_bc[:, :].unsqueeze(1).to_broadcast([128, 2, 256]), op=ALU.add), reads=[PS(bk)] + SETUP, writes=[xk])
                P.add("act", lambda e, hf=hf: e.activation(out=xl, in_=xl, func=AF.Exp, scale=-1.0), reads=[xk], writes=[xk])
                P.add("act", lambda e, hf=hf: e.activation(out=spv[:, 2 * hf:2 * hf + 2, :], in_=xl, func=AF.Ln, scale=1.0, bias=1.0), reads=[xk], writes=["spv%d" % hf])
            for pr in range(2):
                for s in range(4):
                    mm(pbk[pr][:, s * 128:(s + 1) * 128], spv[:, s, pr * 128:(pr + 1) * 128], mfs[:, :], pr, True, True, ["spv%d" % (s // 2)] + SETUP)
            for s in range(4):
                bk = 3 + s // 2
                mm(pbk[bk][:, (s % 2) * 256:(s % 2) * 256 + 256], mbs[:, :], spv[:, s, :], bk, True, True, ["spv%d" % (s // 2)] + SETUP)
            for pr in range(2):
                P.add("act", lambda e, pr=pr: e.activation(out=eb[:, pr, :], in_=pbk[pr][:, :], func=AF.Exp), reads=[PS(pr)], writes=["eb%d" % pr])
                P.add("act", lambda e, pr=pr: e.activation(out=enb[:, pr, :], in_=pbk[pr][:, :], func=AF.Exp, scale=-1.0), reads=[PS(pr)], writes=["enb%d" % pr])
                P.add("act", lambda e, pr=pr: e.activation(out=dec[:, pr, :], in_=pbk[pr][:, :].rearrange("p (c t) -> p c t", t=64)[:, :, 63], func=AF.Exp), reads=[PS(pr)], writes=["dec"])
            for hf in range(2):
                bk = 3 + hf
                P.add("act", lambda e, hf=hf, bk=bk: e.activation(out=edec[:, 2 * hf:2 * hf + 2, :], in_=pbk[bk][:, :].rearrange("p (s n) -> p s n", s=2), func=AF.Exp), reads=[PS(bk)], writes=["edec%d" % hf])

            slot, wv = wload_fm(win_bf, 0)
            for pr in range(2):
                bk = pr
                proj_fm(slot, wv, pr * 128, bk)
                P.add("dve", lambda e, pr=pr, bk=bk: e.scalar_tensor_tensor(out=qf[:, pr, :], in0=pbk[bk][:, :], scalar=0.125, in1=eb[:, pr, :], op0=ALU.mult, op1=ALU.mult), reads=[PS(bk), "eb%d" % pr], writes=["qf%d" % pr])
                P.add("dve", lambda e, pr=pr, bk=bk: e.scalar_tensor_tensor(out=qe[:, pr, :], in0=pbk[bk][:, :], scalar=0.125, in1=enb[:, pr, :], op0=ALU.mult, op1=ALU.mult), reads=[PS(bk), "enb%d" % pr], writes=["qe%d" % pr])
            slot, wv = wload_fm(win_bf, 256)
            for pr in range(2):
                bk = 3 + pr
                proj_fm(slot, wv, pr * 128, bk)
                P.add("dve", lambda e, pr=pr, bk=bk: e.tensor_tensor(out=ke[:, pr, :], in0=pbk[bk][:, :], in1=enb[:, pr, :], op=ALU.mult), reads=[PS(bk), "enb%d" % pr], writes=["ke%d" % pr])
                P.add("dve", lambda e, pr=pr, bk=bk: e.tensor_tensor(out=kb[:, pr, :], in0=pbk[bk][:, :], in1=eb[:, pr, :], op=ALU.mult), reads=[PS(bk), "eb%d" % pr], writes=["kb%d" % pr])
            for s in range(4):
                bk = 5 + (s % 2)
                proj_tm(slot, wv, 0, 256, s, bk)
                for par in range(2):
                    P.add("dve", lambda e, s=s, bk=bk, par=par: e.tensor_tensor(
                        out=kdecP[:, s, par::2, par * 64:par * 64 + 64],
                        in0=pbk[bk][:, 0:256].rearrange("p (h d) -> p h d", h=4)[:, par::2, :],
                        in1=edec[:, s, :].rearrange("p (h d) -> p h d", h=4)[:, par::2, :], op=ALU.mult),
                        reads=[PS(bk), "edec%d" % (s // 2)], writes=["kdecP%d" % s])

            if dbg and g == 0:
                P.add("pool", lambda e: e.dma_start(out=d_sp, in_=spv), reads=["spv0", "spv1"], chan="q_dbg")
                for n_, (a_, k_) in enumerate(((eb, "eb"), (qf, "qf"), (ke, "ke"), (enb, "enb"))):
                    P.add("pool", lambda e: e.dma_start(out=d_g[:, n_ * 2 * T:(n_ + 1) * 2 * T], in_=a_[:, :, :].rearrange("p a t -> p (a t)")), reads=[k_ + "0", k_ + "1"], chan="q_dbg")
                P.add("pool", lambda e: e.dma_start(out=d_gv, in_=gv_tok[:, :, :].rearrange("p a t -> p (a t)")), reads=["gv_tok%d" % k for k in range(4)], chan="q_dbg")
            phase("gla_core")
            AE, AO, OBE, OBO, DSE, DSO, NB = 0, 1, 2, 3, 4, 5, 6
            ob_store = [(Ub[0][:, 0:T], "Ub0"), (Ub[1][:, 0:T], "Ub1"), (fs[6], "fs6"), (fs[7], "fs7")]
            for pr in range(2):
                for s in range(4):
                    sl = slice(s * 128, (s + 1) * 128)
                    for par in range(2):
                        bk = AE if par == 0 else AO
                        rs = slice(par * 64, par * 64 + 64)
                        rk = ["ke%d" % pr, "qf%d" % pr, "kb%d" % pr, "qe%d" % pr]
                        mm(pbk[bk][:, 0:128], ke[rs, pr, sl], qf[rs, pr, sl], bk, True, True, rk)
                        mm(pbk[bk][:, 128:256], kb[rs, pr, sl], qe[rs, pr, sl], bk, True, True, rk)
                    ab = Abf[s % 2]
                    abk = "Abf%d" % (s % 2)
                    for par in range(2):
                        bk = AE if par == 0 else AO
                        i1, i2 = (s % 2) * 4 + par * 2, (s % 2) * 4 + par * 2 + 1
                        t1, t2 = fs[i1], fs[i2]
                        P.add("dve", lambda e, bk=bk, t1=t1: e.tensor_tensor(out=t1[:, 0:128], in0=pbk[bk][:, 0:128], in1=mf[:, :], op=ALU.mult), reads=[PS(bk)] + SETUP, writes=["fs%d" % i1])
                        P.add("dve", lambda e, bk=bk, t2=t2: e.tensor_tensor(out=t2[:, 0:128], in0=pbk[bk][:, 128:256], in1=mb[:, :], op=ALU.mult), reads=[PS(bk)] + SETUP, writes=["fs%d" % i2])
                        P.add("pool", lambda e, t1=t1, t2=t2, ab=ab, par=par: e.tensor_tensor(out=ab[:, par, :], in0=t1[:, 0:128], in1=t2[:, 0:128], op=ALU.add),
                              reads=["fs%d" % i1, "fs%d" % i2], writes=[abk + "_%d" % par])
                    for par in range(2):
                        hh = 2 * pr + par
                        bk = OBE if par == 0 else OBO
                        mm(pbk[bk][:, sl], gv_tok[:, s, hh * 128:(hh + 1) * 128], ab[:, par, :], bk, True, True, ["gv_tok%d" % s, abk + "_%d" % par])
                    for cc in range(2):
                        bk = DSE if cc == 0 else DSO
                        rs = slice(cc * 64, cc * 64 + 64)
                        for par in range(2):
                            hh = 2 * pr + par
                            mm(pbk[bk][:, par * 128:(par + 1) * 128], kdecP[rs, s, hh, :], gv_tok[rs, s, hh * 128:(hh + 1) * 128], bk, True, True, ["kdecP%d" % s, "gv_tok%d" % s])
                        c = 2 * s + cc
                        for par in range(2):
                            ps_ = slice(par * 64, par * 64 + 64)
                            P.add("dve", lambda e: e.scalar_tensor_tensor(out=S[ps_, (c + 1) % 2, pr, :], in0=S[ps_, c % 2, pr, :], scalar=dec[ps_, pr, c:c + 1], in1=pbk[bk][ps_, par * 128:(par + 1) * 128], op0=ALU.mult, op1=ALU.add),
                                  reads=["S%d_%d" % (pr, c % 2), "dec", PS(bk)], writes=["S%d_%d" % (pr, (c + 1) % 2)])
                        P.add("pool", lambda e, pr=pr, c=c: e.tensor_copy(out=Sbf[:, c + 1, pr, :], in_=S[:, (c + 1) % 2, pr, :]), reads=["S%d_%d" % (pr, (c + 1) % 2)], writes=["Sbf%d_%d" % (pr, c + 1)])
                for c in range(8):
                    for par in range(2):
                        bk = AE if par == 0 else AO
                        rs = slice(par * 64, par * 64 + 64)
                        mm(pbk[bk][:, c * 64:(c + 1) * 64], Sbf[rs, c, pr, :], qf[rs, pr, c * 64:(c + 1) * 64], bk, True, True, ["Sbf%d_%d" % (pr, c), "qf%d" % pr])
                P.add("pool", lambda e, pr=pr: e.tensor_copy(out=Sbf[:, 0, pr, :], in_=Sbf[:, 8, pr, :]), reads=["Sbf%d_8" % pr], writes=["Sbf%d_0" % pr])
                for par in range(2):
                    hh = 2 * pr + par
                    bk = OBE if par == 0 else OBO
                    ob, obk = ob_store[hh]
                    P.add("act", lambda e, bk=bk, ob=ob: e.copy(out=ob[:], in_=pbk[bk][:, :]), reads=[PS(bk)], writes=[obk])
                    bi = AE if par == 0 else AO
                    P.add("dve", lambda e: e.tensor_tensor(out=ob[:], in0=pbk[bi][:, :], in1=ob[:], op=ALU.add), reads=[PS(bi), obk], writes=[obk])

            def gla_epilogue(hh):
                ob, obk = ob_store[hh]
                head_norm(ob[:], obk, gnT[:, hh:hh + 1], oT[:, hh, :], ["oT%d" % hh], sg[:, hh, :], "sg%d" % hh, 0.5, 6,
                          fs[4 + hh % 2], "fs%d" % (4 + hh % 2), bs[4 + hh % 2], "bs%d" % (4 + hh % 2))

            phase("dqdk")
            for which, cbase in enumerate((1552, 2064)):
                for hd in range(4):
                    if hd % 2 == 0:
                        sl_, w_ = wload_fm(win_bf, cbase + 128 * hd)
                    bk = hd % 2
                    pbn = 4 + bk
                    proj_fm(sl_, w_, bk * 128, pbn)
                    zb = bs[2 + bk]
                    zk = "bs%d" % (2 + bk)
                    P.add("act", lambda e: e.copy(out=zb[:], in_=pbk[pbn][:, :]), reads=[PS(pbn)], writes=[zk])
                    rb = 7
                    mm(pbk[rb][:, :], perm_bf[:, :], zb[:], rb, True, True, [zk] + SETUP)
                    t1, t2 = fs[bk * 2], fs[bk * 2 + 1]
                    P.add("dve", lambda e, zb=zb, t1=t1: e.tensor_tensor(out=t1[:], in0=zb[:], in1=cosT[:], op=ALU.mult), reads=[zk, "cosT"], writes=["fs%d" % (bk * 2)])
                    P.add("dve", lambda e, rb=rb, t2=t2: e.tensor_tensor(out=t2[:], in0=pbk[rb][:, :], in1=sinT[:], op=ALU.mult), reads=[PS(rb), "sinT"], writes=["fs%d" % (bk * 2 + 1)])
                    if which == 0:
                        dst, dk_ = qT[:, hd, :], "qT%d" % hd
                    else:
                        dst, dk_ = KT[:, hd, i * T:(i + 1) * T], "KT%d" % hd
                    P.add("pool", lambda e, t1=t1, t2=t2, dst=dst: e.tensor_tensor(out=dst, in0=t1[:], in1=t2[:], op=ALU.add),
                          reads=["fs%d" % (bk * 2), "fs%d" % (bk * 2 + 1)], writes=[dk_])
                    if hd % 2 == 1:
                        gla_epilogue(which * 2 + hd // 2)
            for half in range(2):
                slot, wv4 = wload_tm(win_bf, 2576, 4 * half)
                for s in range(4):
                    tm_half(slot, wv4, 4 * half, s, s)
            for s in range(4):
                P.add("act", lambda e: e.copy(out=V[:, i * 4 + s, :], in_=pbk[s][:, :]), reads=[PS(s)], writes=["V"])

            phase("attn")
            SAb, SBb, OA0, OA1, ZA0, ZA1 = (0, 2), (1, 3), 4, 5, 6, 7
            cnt = 0
            deferred = []

            def attn_deferred():
                while deferred:
                    hd_ = deferred.pop(0)
                    P.add("dve", lambda e: e.tensor_tensor(out=fs[6][:], in0=fs[1][:], in1=fs[4][:], op=ALU.mult), reads=["fs1", "fs4"], writes=["fs6"])
                    P.add("dve", lambda e: e.tensor_tensor(out=fs[5][:], in0=fs[2][:], in1=fs[5][:], op=ALU.mult), reads=["fs2", "fs5"], writes=["fs5"])
                    P.add("dve", lambda e: e.scalar_tensor_tensor(out=fs[7][:], in0=fs[5][:], scalar=nlam[:, 0:1], in1=fs[6][:], op0=ALU.mult, op1=ALU.add), reads=["fs5", "fs6", "nlam"], writes=["fs7"])
                    head_norm(fs[7][:], "fs7", dn8[:, hd_:hd_ + 1], oT[:, 4 + hd_, :], ["oT%d" % (4 + hd_)], None, None, None, ZA0,
                              fs[4], "fs4", bs[4], "bs4")

            for hd in range(4):
                ktiles = [(j, 0) for j in range(4 * i)] + [(4 * i + kt, 128 * kt) for kt in range(4)]
                if i == 0:
                    pass
                nk = len(ktiles)
                pend = None
                pacc = (fs[0], fs[1])
                P.add("pool", lambda e: e.memset(pacc[0][:], 0.0), writes=["fs0"])
                for n in range(nk + 1):
                    cur = None
                    if n < nk:
                        j, c0 = ktiles[n]
                        pp = cnt % 2
                        cnt += 1
                        cs = slice(c0, T)
                        ks = slice(j * 128, (j + 1) * 128)
                        mm(pbk[SAb[pp]][:, cs], KT[0:64, hd, ks], qT[0:64, hd, cs], SAb[pp], True, True, ["KT%d" % hd, "qT%d" % hd])
                        mm(pbk[SBb[pp]][:, cs], KT[64:128, hd, ks], qT[64:128, hd, cs], SBb[pp], True, True, ["KT%d" % hd, "qT%d" % hd])
                        P0, P1 = bs[2 * pp], bs[2 * pp + 1]
                        k0, k1 = "bs%d" % (2 * pp), "bs%d" % (2 * pp + 1)
                        P.add("act", lambda e: e.activation(out=Pb[pp][:, :, cs], in_=pb2[pp][:, :].rearrange("p (m t) -> p m t", m=2)[:, :, cs], func=AF.Exp, scale=0.125),
                              reads=[PS(SAb[pp]), PS(SBb[pp])], writes=[k0, k1])
                        if j >= 4 * i:
                            P.add("pool", lambda e: e.memset(Pb[pp][64:128, :, c0:c0 + 64], 0.0), writes=[k0, k1])
                        P.add("dve", lambda e: e.tensor_tensor(out=pacc[0][:, cs], in0=pacc[0][:, cs], in1=P0[:, cs], op=ALU.add), reads=[k0, "fs0"], writes=["fs0"])
                        cur = (n, j, cs, P0, P1, k0, k1)
                    if pend is not None:
                        (n_, j_, cs_, Q0, Q1, q0, q1) = pend
                        st, sp_ = (n_ == 0), (n_ == nk - 1)
                        vv = V[:, j_, hd * 128:(hd + 1) * 128]
                        mm(pbk[OA0][:, cs_], vv, Q0[:, cs_], OA0, st, sp_, ["V", q0])
                        mm(pbk[OA1][:, cs_], vv, Q1[:, cs_], OA1, st, sp_, ["V", q1])
                        mm(pbk[ZA1][:, cs_], ones_bf[:, :], Q1[:, cs_], ZA1, st, sp_, ["ones_bf", q1])
                    pend = cur
                    if n == 2:
                        attn_deferred()
                mm(pbk[ZA0][:, :], ones_f[:, :], pacc[0][:], ZA0, True, True, ["ones_f", "fs0"])
                P.add("act", lambda e: e.copy(out=fs[1][:], in_=pbk[OA0][:, :]), reads=[PS(OA0)], writes=["fs1"])
                P.add("act", lambda e: e.copy(out=fs[2][:], in_=pbk[OA1][:, :]), reads=[PS(OA1)], writes=["fs2"])
                P.add("act", lambda e: e.activation(out=fs[4][:], in_=pbk[ZA0][:, :], func=AF.Ln), reads=[PS(ZA0)], writes=["fs4"])
                P.add("act", lambda e: e.activation(out=fs[5][:], in_=pbk[ZA1][:, :], func=AF.Ln), reads=[PS(ZA1)], writes=["fs5"])
                P.add("act", lambda e: e.activation(out=fs[4][:], in_=fs[4][:], func=AF.Exp, scale=-1.0), reads=["fs4"], writes=["fs4"])
                P.add("act", lambda e: e.activation(out=fs[5][:], in_=fs[5][:], func=AF.Exp, scale=-1.0), reads=["fs5"], writes=["fs5"])
                deferred.append(hd)
            attn_deferred()

            if dbg and g == 0:
                P.add("pool", lambda e: e.dma_start(out=d_oT, in_=oTf[:, :]), reads=["oT%d" % k for k in range(8)], chan="q_dbg")
                P.add("pool", lambda e: e.dma_start(out=d_qk[:, 0:4 * T], in_=qT[:, :, :].rearrange("p a t -> p (a t)")), reads=["qT%d" % k for k in range(4)], chan="q_dbg")
                P.add("pool", lambda e: e.dma_start(out=d_qk[:, 4 * T:5 * T], in_=KT[:, 0, 0:T]), reads=["KT0"], chan="q_dbg")
            if g == 0:
                P.wait_all("pool", ["q_cast"])
                P.add("pool", lambda e: e.memset(marker[:], 0.0), writes=["setup2"])
            phase("wout")
            for hf in range(2):
                for half in range(2):
                    slot, wv4 = wload_tm(wout_bf, hf * 512, 4 * half)
                    for s in range(4):
                        tm_half(slot, wv4, 4 * half, s, hf * 4 + s, act=oT, akeys="oT%d")
                for s in range(4):
                    bk = hf * 4 + s
                    P.add("dve", lambda e, s=s, hf=hf, bk=bk: e.tensor_tensor(out=h[:, s, hf * 512:(hf + 1) * 512], in0=pbk[bk][:, :], in1=h[:, s, hf * 512:(hf + 1) * 512], op=ALU.add),
                          reads=[PS(bk), "h%d" % s], writes=["h%d" % s])

            if dbg and g == 0:
                P.add("sp", lambda e: e.dma_start(out=d_h, in_=h[:, :, :]), reads=["h0", "h1", "h2", "h3"], chan="q_dbg2")
            phase("ffn_norm")
            norm_T(gT2, "gT2")
            phase("ffn_up")
            SQC = math.sqrt(0.044715)
            blocks = {}

            def ffn_A(j):
                gb, q = j // 2, j % 2
                if q == 0:
                    blocks[gb] = (wload_fm(wup_bf, gb * 256), wload_fm(wup_bf, DFF + gb * 256))
                for which in range(2):
                    sl_, w_ = blocks[gb][which]
                    bk = (j % 2) * 2 + which
                    proj_fm(sl_, w_, q * 128, bk)

            def ffn_B(j):
                for which in range(2):
                    ch = j + 22 * which
                    bk = (j % 2) * 2 + which
                    ub = Ub[which]
                    uk = "Ub%d" % which
                    fi = (j % 3) * 2 + which
                    cbuf, ck = fs[fi], "fs%d" % fi
                    P.add("pool", lambda e: e.tensor_copy(out=ub[:, 0:2], in_=HB[:, ch, :]), reads=["HB%d" % ch, "HB"], writes=[uk])
                    P.add("act", lambda e: e.copy(out=ub[:, 2:T + 2], in_=pbk[bk][:, :]), reads=[PS(bk)], writes=[uk])
                    P.add("act", lambda e: e.activation(out=cbuf[:], in_=pbk[bk][:, :], func=AF.Identity, scale=cwT[:, 2, ch:ch + 1], bias=cbT[:, ch:ch + 1]),
                          reads=[PS(bk), "cwh"] + SETUP, writes=[ck])
                    P.add("dve", lambda e: e.scalar_tensor_tensor(out=cbuf[:], in0=ub[:, 1:T + 1], scalar=cwT[:, 1, ch:ch + 1], in1=cbuf[:], op0=ALU.mult, op1=ALU.add),
                          reads=[uk, ck, "cwh"] + SETUP, writes=[ck])
                    P.add("dve", lambda e: e.scalar_tensor_tensor(out=cbuf[:], in0=ub[:, 0:T], scalar=cwT[:, 0, ch:ch + 1], in1=cbuf[:], op0=ALU.mult, op1=ALU.add),
                          reads=[uk, ck, "cwh"] + SETUP, writes=[ck])
                    P.add("pool", lambda e: e.tensor_copy(out=HB[:, ch, :], in_=ub[:, T:T + 2]), reads=[uk], writes=["HB%d" % ch])

            def ffn_bufs(j):
                base = (j % 3) * 2
                ti = 6 + (j % 2)
                return (fs[base], fs[base + 1], fs[ti], "fs%d" % base, "fs%d" % (base + 1), "fs%d" % ti)

            def ffn_C1(j):
                cg, cv, tmp, kcg, kcv, ktm = ffn_bufs(j)
                P.add("act", lambda e: e.activation(out=tmp[:], in_=cg[:], func=AF.Square, scale=SQC), reads=[kcg], writes=[ktm])
                P.add("dve", lambda e: e.scalar_tensor_tensor(out=tmp[:], in0=tmp[:], scalar=1.0, in1=cg[:], op0=ALU.add, op1=ALU.mult), reads=[ktm, kcg], writes=[ktm])
                P.add("pool", lambda e: e.tensor_tensor(out=cv[:], in0=cg[:], in1=cv[:], op=ALU.mult), reads=[kcg, kcv], writes=[kcv])

            def ffn_C2(j):
                cg, cv, tmp, kcg, kcv, ktm = ffn_bufs(j)
                P.add("act", lambda e: e.activation(out=tmp[:], in_=tmp[:], func=AF.Tanh, scale=0.7978845608028654), reads=[ktm], writes=[ktm])
                P.add("dve", lambda e: e.scalar_tensor_tensor(out=actT[:, j, :], in0=tmp[:], scalar=1.0, in1=cv[:], op0=ALU.add, op1=ALU.mult), reads=[ktm, kcv], writes=["actT%d" % j])

            wdv = wdn_bf.rearrange("(c p) n -> p c n", p=128)
            dblk = {}

            def ffn_D(j):
                if j % 2 == 0:
                    dblk[0] = wload(lambda r: r[:, 0:2048].rearrange("p (c n) -> p c n", c=2), wdv[:, j:j + 2, :])
                slot, wv = dblk[0]
                for s in range(2):
                    for hf in range(2):
                        bk = 4 + s * 2 + hf
                        mm(pbk[bk][:, :], actT[:, j, s * 128:(s + 1) * 128], wv[:, j % 2, hf * 512:(hf + 1) * 512], bk, j == 0, j == 21, [("ring", slot), "actT%d" % j])

            ffn_A(0)
            for t in range(25):
                if t + 1 < 22:
                    ffn_A(t + 1)
                if t < 22:
                    ffn_B(t)
                if 0 <= t - 1 < 22:
                    ffn_C1(t - 1)
                if 0 <= t - 2 < 22:
                    ffn_C2(t - 2)
                if 0 <= t - 3 < 22:
                    ffn_D(t - 3)

            def wd_evac(s, bk, hf):
                P.add("dve", lambda e: e.tensor_tensor(out=h[:, s, hf * 512:(hf + 1) * 512], in0=pbk[bk][:, :], in1=h[:, s, hf * 512:(hf + 1) * 512], op=ALU.add),
                      reads=[PS(bk), "h%d" % s], writes=["h%d" % s])

            phase("w_down")
            for s in range(2):
                for hf in range(2):
                    wd_evac(s, 4 + s * 2 + hf, hf)
            for bl in range(11):
                slot, wv = wload(lambda r: r[:, 0:2048].rearrange("p (c n) -> p c n", c=2), wdv[:, bl * 2:bl * 2 + 2, :])
                for fl in range(2):
                    f = bl * 2 + fl
                    for s in (2, 3):
                        for hf in range(2):
                            bk = (s - 2) * 2 + hf
                            mm(pbk[bk][:, :], actT[:, f, s * 128:(s + 1) * 128], wv[:, fl, hf * 512:(hf + 1) * 512], bk, f == 0, f == 21, [("ring", slot), "actT%d" % f])
            for s in (2, 3):
                for hf in range(2):
                    wd_evac(s, (s - 2) * 2 + hf, hf)

            phase("ple")
            norm_T(gT3, "gT3")
            for s in range(4):
                tb = 6 + (s % 2)
                tv = trv[s % 2]
                for kc in range(2):
                    P.add("pe", lambda e, kc=kc, s=s, tv=tv: e.transpose(tv[:, kc * 128:(kc + 1) * 128], p_bf[:, s, kc * 128:(kc + 1) * 128], ident_bf[:]),
                          reads=["p_bf"] + SETUP, writes=[PS(tb)])
                P.add("dve", lambda e, s=s, tv=tv: e.tensor_copy(out=pT[:, :, s * 128:(s + 1) * 128], in_=tv[:, 0:256].rearrange("p (kc t) -> p kc t", kc=2)), reads=[PS(tb)], writes=["qT0", "qT1"])
            for hf in range(2):
                for half in range(2):
                    slot, wv4 = wload_tm(wpg_bf, hf * 512, 4 * half)
                    for s in range(4):
                        tm_half(slot, wv4, 4 * half, s, s)
                for s in range(4):
                    bg = s
                    bp = 4 + (s % 2)
                    for kc in range(2):
                        mm(pbk[bp][:, :], pT[:, kc, s * 128:(s + 1) * 128], wpp[:, kc, hf * 512:(hf + 1) * 512], bp, kc == 0, kc == 1, ["qT0", "qT1"] + SETUP)
                    th = fs[s]
                    tk = "fs%d" % s
                    P.add("act", lambda e: e.activation(out=th[:], in_=pbk[bg][:, :], func=AF.Tanh, scale=0.5), reads=[PS(bg)], writes=[tk])
                    P.add("dve", lambda e: e.scalar_tensor_tensor(out=th[:], in0=th[:], scalar=1.0, in1=pbk[bp][:, :], op0=ALU.add, op1=ALU.mult), reads=[tk, PS(bp)], writes=[tk])
                    P.add("dve", lambda e: e.scalar_tensor_tensor(out=h[:, s, hf * 512:(hf + 1) * 512], in0=th[:], scalar=0.5, in1=h[:, s, hf * 512:(hf + 1) * 512], op0=ALU.mult, op1=ALU.add),
                          reads=[tk, "h%d" % s], writes=["h%d" % s])

            phase("final")
            P.add("pool", lambda e: e.memset(ssq[:], 0.0), writes=["ssq"])
            for s in range(4):
                P.add("act", lambda e, s=s: e.activation(out=junk[:], in_=h[:, s, :], func=AF.Square, accum_out=ssq[:, s:s + 1]), reads=["h%d" % s, "ssq"], writes=["hn_tok1", "ssq"])
            rstd_from(ssq[:], rstd[:], 4, 1.0 / 1024.0, ["ssq"], "rstd", lnv[:], "lnv")
            for s in range(4):
                P.add("dve", lambda e, s=s: e.scalar_tensor_tensor(out=h[:, s, :], in0=h[:, s, :], scalar=rstd[:, s:s + 1], in1=gf_bc[:, :], op0=ALU.mult, op1=ALU.mult),
                      reads=["h%d" % s, "rstd"] + SETUP, writes=["h%d" % s])
                P.add("sp", lambda e, s=s: e.dma_start(out=y_d[tok0 + s * 128: tok0 + (s + 1) * 128, :], in_=h[:, s, :]), reads=["h%d" % s], chan="q_y%d" % s)

        P.wait_all("sp", ["q_y0", "q_y1", "q_y2", "q_y3", "q_dbg", "q_dbg2"])
        P.emit()
    return nc


_W_KEYS = ["w_in", "w_out", "w_up", "w_down", "w_ple_gate", "w_ple_proj", "w_a_up"]
_V_KEYS = ["b_a", "norm_mix", "norm_ffn", "norm_ple", "gla_norm", "diff_norm", "conv_b",
           "lam_q1", "lam_k1", "lam_q2", "lam_k2"]


def kernel(**inputs):
    x = np.asarray(inputs["x"], dtype=np.float32)
    p = np.asarray(inputs["p"], dtype=np.float32)[0]
    pos = np.asarray(inputs["positions"]).astype(np.int32)
    shared = {}
    for k in _W_KEYS:
        shared[k] = np.ascontiguousarray(np.asarray(inputs[k], dtype=np.float32)[0])
    for k in _V_KEYS:
        shared[k] = np.ascontiguousarray(np.asarray(inputs[k], dtype=np.float32).reshape(1, -1))
    shared["conv_w"] = np.ascontiguousarray(np.asarray(inputs["conv_w"], dtype=np.float32)[0])
    shared["norm_final"] = np.ascontiguousarray(np.asarray(inputs["norm_final"], dtype=np.float32).reshape(1, -1))
    shared.update(host_consts())
    in_maps = []
    for c in range(NCORES):
        m = dict(shared)
        m["x"] = np.ascontiguousarray(x[c * NSEQ:(c + 1) * NSEQ].reshape(NSEQ * SEQ, 1024))
        m["p"] = np.ascontiguousarray(p[c * NSEQ:(c + 1) * NSEQ].reshape(NSEQ * SEQ, 256))
        m["pos"] = np.ascontiguousarray(pos[c * NSEQ:(c + 1) * NSEQ].reshape(1, NSEQ * SEQ))
        in_maps.append(m)
    nc = build_program()
    res = run_bass_kernel_spmd(nc, in_maps, core_ids=list(range(NCORES)))
    out = np.concatenate([np.asarray(r["y"]).reshape(NSEQ, SEQ, 1024) for r in res.results], axis=0)
    return out.astype(np.float32)
```

```python
import bisect
import contextlib
import math
import numpy as np
import concourse.bass as bass
import concourse.mybir as mybir
from concourse.bass_utils import run_bass_kernel_spmd

F32 = mybir.dt.float32
BF16 = mybir.dt.bfloat16
I32 = mybir.dt.int32
AF = mybir.ActivationFunctionType
ALU = mybir.AluOpType

COMPUTE = ("pe", "act", "dve", "pool")
MARKS = []
NCORES = 8
SEQ = 4096
NSEQ = 2
T = 512
NTI = SEQ // T
EPS = 1e-6
DFF = 2816
NSLOT = 4


class _Rec:
    def __getattr__(self, name):
        def f(*a, **k):
            self.call = (name, a, k)
            return self
        return f


class Prog:
    def __init__(self, nc):
        self.nc = nc
        self.streams = {s: [] for s in ("pe", "act", "dve", "pool", "sp")}
        self.count = {}
        self.res = {}
        self.seen = {s: {} for s in self.streams}
        self.need = {}
        self.dma_chans = set()

    def _st(self, key):
        st = self.res.get(key)
        if st is None:
            st = self.res[key] = {"w": None, "r": []}
        return st

    def add(self, stream, fn, reads=(), writes=(), chan=None):
        rec = _Rec()
        fn(rec)
        call = rec.call
        fn = lambda e, call=call: getattr(e, call[0])(*call[1], **call[2])
        if chan is None:
            chan = stream
        else:
            self.dma_chans.add(chan)
        idx = self.count.get(chan, 0) + 1
        self.count[chan] = idx
        me = (chan, idx)
        deps = set()
        for key in reads:
            st = self._st(key)
            if st["w"] is not None:
                deps.add(st["w"])
            if isinstance(key, tuple) and key[0] == "ps":
                for r in st["r"]:
                    if r[0] != chan:
                        deps.add(r)
        for key in writes:
            st = self._st(key)
            if st["w"] is not None:
                deps.add(st["w"])
            deps.update(st["r"])
        for key in reads:
            self._st(key)["r"].append(me)
        for key in writes:
            st = self._st(key)
            st["w"] = me
            st["r"] = []
        waits = []
        seen = self.seen[stream]
        best = {}
        for (c, n) in deps:
            if c == chan:
                if chan == "pe":
                    continue
                if chan in COMPUTE and idx - n > 1:
                    continue
            if n > best.get(c, 0):
                best[c] = n
        for c in sorted(best):
            n = best[c]
            if seen.get(c, 0) >= n:
                continue
            seen[c] = n
            waits.append((c, n))
            self.need.setdefault(c, set()).add(n)
        self.streams[stream].append((fn, waits, chan, idx))
        return me

    def wait_all(self, stream, chans):
        w = []
        for c in chans:
            n = self.count.get(c, 0)
            if n == 0:
                continue
            w.append((c, n))
            self.need.setdefault(c, set()).add(n)
            self.seen[stream][c] = max(self.seen[stream].get(c, 0), n)
        self.streams[stream].append((None, w, None, None))

    def emit(self):
        nc = self.nc
        chans = sorted(set(self.count.keys()))
        sems = {}
        with contextlib.ExitStack() as es:
            for c in chans:
                sems[c] = es.enter_context(nc.semaphore("s_" + c))
            needl = {c: sorted(v) for c, v in self.need.items()}

            def val(c, n):
                if c in self.dma_chans:
                    return 16 * n
                return bisect.bisect_right(needl[c], n)

            def run(stream, eng):
                for (fn, waits, chan, idx) in self.streams[stream]:
                    for (c, n) in waits:
                        v = val(c, n)
                        if v > 0:
                            eng.wait_ge(sems[c], v)
                    if fn is None:
                        continue
                    ins = fn(eng)
                    if chan in self.dma_chans:
                        ins.then_inc(sems[chan], 16)
                    elif idx in self.need.get(chan, ()):
                        ins.then_inc(sems[chan], 1)

            with nc.Block() as block:
                @block.sync
                def _(e):
                    run("sp", e)

                @block.scalar
                def _(e):
                    run("act", e)

                @block.tensor
                def _(e):
                    run("pe", e)

                @block.vector
                def _(e):
                    run("dve", e)

                @block.gpsimd
                def _(e):
                    run("pool", e)


def host_consts():
    s = np.arange(128)[:, None]
    t = np.arange(128)[None, :]
    same = (s // 64) == (t // 64)
    mf = (same & (s <= t)).astype(np.float32)
    mb = (same & (s > t)).astype(np.float32)
    pidx = np.arange(128)
    perm = np.zeros((128, 128), np.float32)
    src = np.where((pidx % 64) < 32, pidx + 32, pidx - 32)
    perm[src, pidx] = 1.0
    j = (pidx % 32).astype(np.float32)
    invf = (np.float32(10000.0) ** (-(2.0 * j) / np.float32(64.0))).astype(np.float32)
    sgn = np.where((pidx % 64) < 32, -1.0, 1.0).astype(np.float32)
    return {
        "c_ident": np.eye(128, dtype=np.float32),
        "c_perm": perm,
        "c_mf": mf, "c_mb": mb,
        "c_mfs": (-mf / 16.0).astype(np.float32), "c_mbs": (-mb / 16.0).astype(np.float32),
        "c_invf": invf.reshape(128, 1), "c_sgn": sgn.reshape(128, 1),
    }


def build_program(ntiles_total=NSEQ * NTI, dbg=False):
    nc = bass.Bass("TRN2", target_bir_lowering=False)

    def din(name, shape, dt=F32):
        return nc.dram_tensor(name, shape, dt, kind="ExternalInput").ap()

    NTOK = NSEQ * SEQ
    x_d = din("x", [NTOK, 1024])
    p_d = din("p", [NTOK, 256])
    pos_d = din("pos", [1, NTOK], I32)
    w_in_d = din("w_in", [1024, 3088])
    w_out_d = din("w_out", [1024, 1024])
    w_up_d = din("w_up", [1024, 5632])
    w_dn_d = din("w_down", [DFF, 1024])
    w_pg_d = din("w_ple_gate", [1024, 1024])
    w_pp_d = din("w_ple_proj", [256, 1024])
    w_au_d = din("w_a_up", [16, 256])
    b_a_d = din("b_a", [1, 256])
    n_mix_d = din("norm_mix", [1, 1024])
    n_ffn_d = din("norm_ffn", [1, 1024])
    n_ple_d = din("norm_ple", [1, 1024])
    n_fin_d = din("norm_final", [1, 1024])
    g_norm_d = din("gla_norm", [1, 512])
    d_norm_d = din("diff_norm", [1, 512])
    cw_d = din("conv_w", [3, 5632])
    cb_d = din("conv_b", [1, 5632])
    lam_d = [din(n, [1, 64]) for n in ("lam_q1", "lam_k1", "lam_q2", "lam_k2")]
    c_ident_d = din("c_ident", [128, 128])
    c_perm_d = din("c_perm", [128, 128])
    c_mf_d = din("c_mf", [128, 128])
    c_mb_d = din("c_mb", [128, 128])
    c_mfs_d = din("c_mfs", [128, 128])
    c_mbs_d = din("c_mbs", [128, 128])
    c_invf_d = din("c_invf", [128, 1])
    c_sgn_d = din("c_sgn", [128, 1])
    y_d = nc.dram_tensor("y", [NTOK, 1024], F32, kind="ExternalOutput").ap()

    win_bf = nc.dram_tensor("win_bf", [1024, 3088], BF16, kind="Internal").ap()
    wout_bf = nc.dram_tensor("wout_bf", [1024, 1024], BF16, kind="Internal").ap()
    wup_bf = nc.dram_tensor("wup_bf", [1024, 5632], BF16, kind="Internal").ap()
    wdn_bf = nc.dram_tensor("wdn_bf", [DFF, 1024], BF16, kind="Internal").ap()
    wpg_bf = nc.dram_tensor("wpg_bf", [1024, 1024], BF16, kind="Internal").ap()

    if dbg:
        d_oT = nc.dram_tensor("d_oT", [128, 8 * T], F32, kind="ExternalOutput").ap()
        d_h = nc.dram_tensor("d_h", [128, 4, 1024], F32, kind="ExternalOutput").ap()
        d_qk = nc.dram_tensor("d_qk", [128, 5 * T], F32, kind="ExternalOutput").ap()
        d_sp = nc.dram_tensor("d_sp", [128, 4, 256], F32, kind="ExternalOutput").ap()
        d_g = nc.dram_tensor("d_g", [128, 8 * T], F32, kind="ExternalOutput").ap()
        d_gv = nc.dram_tensor("d_gv", [128, 4 * 512], F32, kind="ExternalOutput").ap()
    P = Prog(nc)
    with contextlib.ExitStack() as es:
        def sb(name, shape, dt=F32):
            return es.enter_context(nc.sbuf_tensor(name, shape, dt))

        pb2 = [es.enter_context(nc.psum_tensor("pb%d" % i, [128, 1024], F32)) for i in range(4)]
        pbk = []
        for i in range(4):
            pbk.append(pb2[i][:, 0:512])
            pbk.append(pb2[i][:, 512:1024])

        def PS(b):
            return ("ps", b)

        h = sb("h", [128, 4, 1024])
        KT = sb("KT", [128, 4, SEQ], BF16)
        V = sb("V", [128, SEQ // 128, 512], BF16)
        actF = sb("actT", [128, 22 * T], BF16)
        actT = actF[:, :].rearrange("p (j t) -> p j t", j=22)
        hnT = sb("hnT", [128, 8, T], BF16)
        oTf = sb("oT", [128, 8 * T], BF16)
        oT = oTf[:, :].rearrange("p (k t) -> p k t", k=8)
        qT = sb("qT", [128, 4, T], BF16)
        ring = [sb("ring%d" % i, [128, 2048], BF16) for i in range(NSLOT)]
        wga = sb("wga", [128, 8, 16], BF16)
        wpp = sb("wpp", [128, 2, 1024], BF16)
        wau = sb("wau", [16, 256], BF16)
        HB = sb("HB", [128, 44, 2])
        gf_bc = sb("gf_bc", [128, 1024])
        ba_bc = sb("ba_bc", [128, 256])
        gT1 = sb("gT1", [128, 8]); gT2 = sb("gT2", [128, 8]); gT3 = sb("gT3", [128, 8])
        gnT = sb("gnT", [128, 4]); dnT = sb("dnT", [128, 4]); dn8 = sb("dn8", [128, 4])
        cwT = sb("cwT", [128, 3, 44]); cbT = sb("cbT", [128, 44])
        lamv = [sb("lamv%d" % i, [128, 64]) for i in range(4)]
        lamt = sb("lamt", [128, 64]); lams = sb("lams", [128, 2]); lame = sb("lame", [128, 2]); nlam = sb("nlam", [128, 1])
        ident_bf = sb("ident_bf", [128, 128], BF16)
        perm_bf = sb("perm_bf", [128, 128], BF16)
        mf = sb("mf", [128, 128]); mb = sb("mb", [128, 128]); mfs = sb("mfs", [128, 128]); mbs = sb("mbs", [128, 128])
        ones_bf = sb("ones_bf", [128, 128], BF16)
        ones_f = sb("ones_f", [128, 128])
        invf = sb("invf", [128, 1]); sgn = sb("sgn", [128, 1])
        marker = sb("marker", [128, 1])
        ssq = sb("ssq", [128, 4]); lnv = sb("lnv", [128, 4]); rstd = sb("rstd", [128, 4])
        hn_tok = [sb("hn_tok0", [128, 1024], BF16), sb("hn_tok1", [128, 1024], BF16)]
        junk = hn_tok[1]
        posf = sb("posf", [128, T])
        cosT = oTf[:, 4 * T:6 * T].bitcast(F32)
        sinT = oTf[:, 6 * T:8 * T].bitcast(F32)
        p_bf = sb("p_bf", [128, 4, 256], BF16)
        pT = qT[:, 0:2, :]
        sg = sb("sg", [128, 4, T], BF16)
        aT = sb("aT", [16, T], BF16)
        enb = sb("enb", [128, 2, T], BF16)
        dec = sb("dec", [128, 2, 8])
        def av(a, n, shp):
            return actF[:, a:a + n].rearrange("p (j t) -> p j t", j=shp)
        qf = av(0, 1024, 2); qe = av(1024, 1024, 2); ke = av(2048, 1024, 2); kb = av(3072, 1024, 2)
        gv_tok = av(4096, 2048, 4)
        eb = av(10240, 1024, 2)
        spv = actF[:, 6144:8192].bitcast(F32).rearrange("p (s n) -> p s n", s=4)
        edec = actF[:, 8192:10240].bitcast(F32).rearrange("p (s n) -> p s n", s=4)
        kdecP = sb("kdecP", [128, 4, 4, 128], BF16)
        Abf = [sb("Abf%d" % i, [128, 4, 128], BF16) for i in range(2)]
        S = sb("S", [128, 2, 2, 128]); Sbf = sb("Sbf", [128, 9, 2, 128], BF16)
        fs = [sb("fs%d" % i, [128, T]) for i in range(8)]
        Pb = [sb("Pb%d" % i, [128, 2, T], BF16) for i in range(2)]
        bs = [Pb[0][:, 0, :], Pb[0][:, 1, :], Pb[1][:, 0, :], Pb[1][:, 1, :], sb("bs4", [128, T], BF16), sb("bs5", [128, T], BF16)]
        Ub = [sb("Ub%d" % i, [128, T + 2]) for i in range(2)]

        trv = [pbk[6][:].bitcast(BF16), pbk[7][:].bitcast(BF16)]

        ccount = [0]

        def cload(dst, src, key, **kw):
            ccount[0] += 1
            P.add("sp", lambda e: e.dma_start(out=dst, in_=src, **kw), writes=["c%d" % ccount[0]], chan="q_const")

        cload(mf[:], c_mf_d, "c"); cload(mb[:], c_mb_d, "c"); cload(mfs[:], c_mfs_d, "c"); cload(mbs[:], c_mbs_d, "c")
        cload(invf[:], c_invf_d, "c"); cload(sgn[:], c_sgn_d, "c")
        cload(gf_bc[:], n_fin_d.partition_broadcast(128), "c")
        cload(ba_bc[:], b_a_d.partition_broadcast(128), "c")
        for gt, d in ((gT1, n_mix_d), (gT2, n_ffn_d), (gT3, n_ple_d)):
            cload(gt[:], d.rearrange("o (kc p) -> p (o kc)", p=128), "c", allow_slow_non_contiguous=True)
        cload(gnT[:], g_norm_d.rearrange("o (h p) -> p (o h)", p=128), "c", allow_slow_non_contiguous=True)
        cload(dnT[:], d_norm_d.rearrange("o (h p) -> p (o h)", p=128), "c", allow_slow_non_contiguous=True)
        cload(cwT[:], cw_d.rearrange("j (c p) -> p j c", p=128), "c", allow_slow_non_contiguous=True)
        cload(cbT[:], cb_d.rearrange("o (c p) -> p (o c)", p=128), "c", allow_slow_non_contiguous=True)
        for i in range(4):
            cload(lamv[i][:], lam_d[i].partition_broadcast(128), "c")

        def castdma(dst, src):
            P.add("pool", lambda e: e.dma_start(out=dst, in_=src), chan="q_cast")

        for k in range(8):
            r0, r1 = k * 128, (k + 1) * 128
            castdma(win_bf[r0:r1, :], w_in_d[r0:r1, :])
            castdma(wup_bf[r0:r1, :], w_up_d[r0:r1, :])
            castdma(wout_bf[r0:r1, :], w_out_d[r0:r1, :])
            castdma(wpg_bf[r0:r1, :], w_pg_d[r0:r1, :])
        for k in range(22):
            r0, r1 = k * 128, (k + 1) * 128
            castdma(wdn_bf[r0:r1, :], w_dn_d[r0:r1, :])
        castdma(wpp[:], w_pp_d.rearrange("(kc p) n -> p kc n", p=128))
        castdma(wau[:], w_au_d)
        castdma(wga[:], w_in_d.rearrange("(kc p) n -> p kc n", p=128)[:, :, 1536:1552])
        castdma(ident_bf[:], c_ident_d)
        castdma(perm_bf[:], c_perm_d)
        P.wait_all("pool", ["q_cast", "q_const"])
        P.add("pool", lambda e: e.memset(marker[:], 0.0), writes=["setup"])
        SETUP = ["setup"]
        P.add("pool", lambda e: e.memset(ones_bf[:], 1.0), writes=["ones_bf"])
        P.add("pool", lambda e: e.memset(ones_f[:], 1.0), writes=["ones_f"])
        P.add("dve", lambda e: e.tensor_scalar(out=dn8[:], in0=dnT[:], scalar1=0.8, scalar2=None, op0=ALU.mult), reads=SETUP, writes=["dn8"])
        P.add("dve", lambda e: e.tensor_scalar(out=cwT[:, :, 22:44], in0=cwT[:, :, 22:44], scalar1=0.5, scalar2=None, op0=ALU.mult), reads=SETUP, writes=["cwh"])
        P.add("dve", lambda e: e.tensor_scalar(out=cbT[:, 22:44], in0=cbT[:, 22:44], scalar1=0.5, scalar2=None, op0=ALU.mult), reads=SETUP, writes=["cwh"])
        for i in range(2):
            P.add("dve", lambda e, i=i: e.tensor_tensor(out=lamt[:], in0=lamv[2 * i][:], in1=lamv[2 * i + 1][:], op=ALU.mult), reads=SETUP, writes=["lamt"])
            P.add("dve", lambda e, i=i: e.tensor_reduce(out=lams[:, i:i + 1], in_=lamt[:], axis=mybir.AxisListType.X, op=ALU.add), reads=["lamt"], writes=["lams"])
        P.add("act", lambda e: e.activation(out=lame[:], in_=lams[:], func=AF.Exp), reads=["lams"], writes=["lame"])
        P.add("dve", lambda e: e.scalar_tensor_tensor(out=nlam[:], in0=lame[:, 1:2], scalar=-0.2, in1=lame[:, 0:1], op0=ALU.add, op1=ALU.subtract), reads=["lame"], writes=["nlam"])

        nload = [0]

        def wload(view_fn, src):
            slot = nload[0] % NSLOT
            nload[0] += 1
            dst = view_fn(ring[slot])
            P.add("sp", lambda e: e.dma_start(out=dst, in_=src), reads=SETUP, writes=[("ring", slot)], chan="q_w%d" % slot)
            return slot, dst

        def wload_fm(src_bf, c0, ncols=256):
            return wload(lambda r: r[:, 0:8 * ncols].rearrange("p (kc n) -> p kc n", kc=8),
                         src_bf.rearrange("(kc p) n -> p kc n", p=128)[:, :, c0:c0 + ncols])

        def wload_tm(src_bf, c0, kc0):
            return wload(lambda r: r[:, 0:2048].rearrange("p (kc n) -> p kc n", kc=4),
                         src_bf.rearrange("(kc p) n -> p kc n", p=128)[:, kc0:kc0 + 4, c0:c0 + 512])

        def mm(out, lhsT, rhs, bank, start, stop, reads):
            P.add("pe", lambda e: e.matmul(out, lhsT, rhs, start=start, stop=stop), reads=reads, writes=[PS(bank)])

        def rstd_from(src, dst, n, inv_n, rkeys, wkey, tmp, tkey):
            P.add("act", lambda e: e.activation(out=tmp, in_=src, func=AF.Ln, scale=inv_n, bias=eps_t[:, 0:1]), reads=rkeys + ["eps_t"], writes=[tkey])
            P.add("act", lambda e: e.activation(out=dst, in_=tmp, func=AF.Exp, scale=-0.5), reads=[tkey], writes=[wkey])

        eps_t = sb("eps_t", [128, 1])
        P.add("pool", lambda e: e.memset(eps_t[:], EPS), writes=["eps_t"])

        def norm_T(gT, gkey):
            P.add("pool", lambda e: e.memset(ssq[:], 0.0), writes=["ssq"])
            for s in range(4):
                P.add("act", lambda e, s=s: e.activation(out=junk[:], in_=h[:, s, :], func=AF.Square, accum_out=ssq[:, s:s + 1]),
                      reads=["h%d" % s, "ssq"], writes=["hn_tok1", "ssq"])
            rstd_from(ssq[:], rstd[:], 4, 1.0 / 1024.0, ["ssq"], "rstd", lnv[:], "lnv")
            for s in range(4):
                ht = hn_tok[s % 2]
                hk = "hn_tok%d" % (s % 2)
                P.add("act", lambda e, s=s, ht=ht: e.activation(out=ht[:], in_=h[:, s, :], func=AF.Identity, scale=rstd[:, s:s + 1]),
                      reads=["h%d" % s, "rstd"], writes=[hk])
                tb = 6 + (s % 2)
                tv = trv[s % 2]
                for kc in range(8):
                    P.add("pe", lambda e, kc=kc, ht=ht, tv=tv: e.transpose(tv[:, kc * 128:(kc + 1) * 128], ht[:, kc * 128:(kc + 1) * 128], ident_bf[:]),
                          reads=[hk] + SETUP, writes=[PS(tb)])
                P.add("dve", lambda e, s=s, tv=tv: e.tensor_tensor(
                    out=hnT[:, :, s * 128:(s + 1) * 128],
                    in0=tv[:, :].rearrange("p (kc t) -> p kc t", kc=8),
                    in1=gT[:, :].unsqueeze(2).to_broadcast([128, 8, 128]), op=ALU.mult),
                    reads=[PS(tb)] + SETUP, writes=["hnT"])

        def proj_fm(slot, wv, c, bank, M=128):
            for kc in range(8):
                mm(pbk[bank][0:M, :], wv[:, kc, c:c + M], hnT[:, kc, :], bank, kc == 0, kc == 7, [("ring", slot), "hnT"])

        def proj_tm(slot, wv, c0, ncols, s, bank):
            for kc in range(8):
                mm(pbk[bank][:, 0:ncols], hnT[:, kc, s * 128:(s + 1) * 128], wv[:, kc, c0:c0 + ncols], bank, kc == 0, kc == 7, [("ring", slot), "hnT"])

        def tm_half(slot, wv4, kbase, s, bank, act=None, akeys=None):
            for k in range(4):
                kc = kbase + k
                if act is None:
                    lhs, rk = hnT[:, kc, s * 128:(s + 1) * 128], ["hnT"]
                else:
                    lhs, rk = act[:, kc, s * 128:(s + 1) * 128], [akeys % kc]
                mm(pbk[bank][:, :], lhs, wv4[:, k, :], bank, kc == 0, kc == 7, [("ring", slot)] + rk)

        def head_norm(src_f32, skey, gcol, dst_bf, dkeys, extra_in1, extra_key, extra_scale, nb_bank, f_a, fa_key, b_a_, ba_key):
            P.add("act", lambda e: e.activation(out=b_a_[:], in_=src_f32, func=AF.Square), reads=[skey], writes=[ba_key])
            mm(pbk[nb_bank][:, :], ones_bf[:], b_a_[:], nb_bank, True, True, [ba_key, "ones_bf"])
            P.add("act", lambda e: e.activation(out=f_a[:], in_=pbk[nb_bank][:, :], func=AF.Ln, scale=1.0 / 128.0, bias=eps_t[:, 0:1]), reads=[PS(nb_bank), "eps_t"], writes=[fa_key])
            P.add("act", lambda e: e.activation(out=f_a[:], in_=f_a[:], func=AF.Exp, scale=-0.5), reads=[fa_key], writes=[fa_key])
            if extra_in1 is None:
                P.add("dve", lambda e: e.scalar_tensor_tensor(out=dst_bf, in0=src_f32, scalar=gcol, in1=f_a[:], op0=ALU.mult, op1=ALU.mult),
                      reads=[skey, fa_key] + SETUP + ["dn8"], writes=dkeys)
            else:
                P.add("dve", lambda e: e.scalar_tensor_tensor(out=f_a[:], in0=src_f32, scalar=gcol, in1=f_a[:], op0=ALU.mult, op1=ALU.mult),
                      reads=[skey, fa_key] + SETUP, writes=[fa_key])
                P.add("dve", lambda e: e.scalar_tensor_tensor(out=dst_bf, in0=f_a[:], scalar=extra_scale, in1=extra_in1, op0=ALU.mult, op1=ALU.mult),
                      reads=[fa_key, extra_key], writes=dkeys)

        def phase(name):
            MARKS.append((len(MARKS), name, P.count.get("pe", 0)))

        for g in range(ntiles_total):
            phase("tile%d" % g)
            b = g // NTI
            i = g % NTI
            tok0 = b * SEQ + i * T
            for s in range(4):
                P.add("sp", lambda e, s=s: e.dma_start(out=h[:, s, :], in_=x_d[tok0 + s * 128: tok0 + (s + 1) * 128, :]), writes=["h%d" % s], chan="q_x%d" % s)
            P.add("pool", lambda e: e.dma_start(out=p_bf[:], in_=p_d[tok0:tok0 + T, :].rearrange("(s p) n -> p s n", p=128)), writes=["p_bf"], chan="q_p")
            P.add("pool", lambda e: e.dma_start(out=posf[:], in_=pos_d[:, tok0:tok0 + T].partition_broadcast(128)), writes=["posf"], chan="q_pos")
            if i == 0:
                P.add("pool", lambda e: e.memset(S[:], 0.0), writes=["S0_0", "S1_0", "S0_1", "S1_1"])
                P.add("pool", lambda e: e.memset(Sbf[:, 0, :, :], 0.0), writes=["Sbf0_0", "Sbf1_0"])
                P.add("pool", lambda e: e.memset(HB[:], 0.0), writes=["HB"])
                if g == 0:
                    P.add("pool", lambda e: e.memset(kdecP[:], 0.0), writes=["kdecP0", "kdecP1", "kdecP2", "kdecP3"])

            ang, angc, kf, rr = fs[0], fs[1], fs[2], fs[3]
            ki = fs[4][:, :].bitcast(I32)
            P.add("dve", lambda e: e.tensor_scalar(out=ang[:], in0=posf[:], scalar1=invf[:, 0:1], scalar2=None, op0=ALU.mult), reads=["posf"] + SETUP, writes=["fs0"])
            P.add("dve", lambda e: e.tensor_scalar(out=angc[:], in0=ang[:], scalar1=math.pi / 2, scalar2=None, op0=ALU.add), reads=["fs0"], writes=["fs1"])
            for (src, skey, dst, dkey, scl) in ((ang, "fs0", sinT, "sinT", sgn), (angc, "fs1", cosT, "cosT", None)):
                P.add("dve", lambda e, src=src: e.tensor_scalar(out=ki, in0=src[:], scalar1=1.0 / (2 * math.pi), scalar2=None, op0=ALU.mult), reads=[skey], writes=["fs4"])
                P.add("dve", lambda e: e.tensor_copy(out=kf[:], in_=ki), reads=["fs4"], writes=["fs2"])
                P.add("dve", lambda e, src=src: e.scalar_tensor_tensor(out=rr[:], in0=kf[:], scalar=-2 * math.pi, in1=src[:], op0=ALU.mult, op1=ALU.add), reads=["fs2", skey], writes=["fs3"])
                P.add("dve", lambda e: e.tensor_scalar(out=rr[:], in0=rr[:], scalar1=math.pi, scalar2=-math.pi, op0=ALU.min, op1=ALU.max), reads=["fs3"], writes=["fs3"])
                if scl is not None:
                    P.add("act", lambda e, dst=dst: e.activation(out=dst[:], in_=rr[:], func=AF.Sin, scale=sgn[:, 0:1]), reads=["fs3"] + SETUP, writes=[dkey])
                else:
                    P.add("act", lambda e, dst=dst: e.activation(out=dst[:], in_=rr[:], func=AF.Sin), reads=["fs3"], writes=[dkey])

            phase("norm1")
            norm_T(gT1, "gT1")

            phase("w_in_gla")
            for half in range(2):
                slot, wv4 = wload_tm(win_bf, 512, 4 * half)
                for s in range(4):
                    tm_half(slot, wv4, 4 * half, s, s)
            for s in range(4):
                P.add("act", lambda e: e.copy(out=gv_tok[:, s, :], in_=pbk[s][:, :]), reads=[PS(s)], writes=["gv_tok%d" % s])

            for half in range(2):
                slot, wv = wload_fm(win_bf, 1024 + 256 * half)
                for q in range(2):
                    hh = 2 * half + q
                    bk = hh % 2
                    proj_fm(slot, wv, q * 128, bk)
                    th = fs[4 + bk]
                    P.add("act", lambda e: e.activation(out=th[:], in_=pbk[bk][:, :], func=AF.Tanh, scale=0.5), reads=[PS(bk)], writes=["fs%d" % (4 + bk)])
                    P.add("dve", lambda e: e.scalar_tensor_tensor(out=sg[:, hh, :], in0=th[:], scalar=1.0, in1=pbk[bk][:, :], op0=ALU.add, op1=ALU.mult),
                          reads=["fs%d" % (4 + bk), PS(bk)], writes=["sg%d" % hh])
            for kc in range(8):
                mm(pbk[2][0:16, :], wga[:, kc, :], hnT[:, kc, :], 2, kc == 0, kc == 7, ["hnT"] + SETUP)
            P.add("act", lambda e: e.copy(out=aT[:], in_=pbk[2][0:16, :]), reads=[PS(2)], writes=["aT"])
            for s in range(4):
                bk = 3 + s // 2
                mm(pbk[bk][:, (s % 2) * 256:(s % 2) * 256 + 256], aT[:, s * 128:(s + 1) * 128], wau[:, :], bk, True, True, ["aT"] + SETUP)
            for hf in range(2):
                bk = 3 + hf
                xl = fs[6 + hf][:, :].rearrange("p (s n) -> p s n", s=2)
                xk = "fs%d" % (6 + hf)
                P.add("dve", lambda e, hf=hf, bk=bk: e.tensor_tensor(
                    out=xl, in0=pbk[bk][:, :].rearrange("p (s n) -> p s n", s=2),
                    in1=ba_bc[:, :].unsqueeze(1).to_broadcast([128, 2, 256]), op=ALU.add), reads=[PS(bk)] + SETUP, writes=[xk])
                P.add("act", lambda e, hf=hf: e.activation(out=xl, in_=xl, func=AF.Exp, scale=-1.0), reads=[xk], writes=[xk])
                P.add("act", lambda e, hf=hf: e.activation(out=spv[:, 2 * hf:2 * hf + 2, :], in_=xl, func=AF.Ln, scale=1.0, bias=1.0), reads=[xk], writes=["spv%d" % hf])
            for pr in range(2):
                for s in range(4):
                    mm(pbk[pr][:, s * 128:(s + 1) * 128], spv[:, s, pr * 128:(pr + 1) * 128], mfs[:, :], pr, True, True, ["spv%d" % (s // 2)] + SETUP)
            for s in range(4):
                bk = 3 + s // 2
                mm(pbk[bk][:, (s % 2) * 256:(s % 2) * 256 + 256], mbs[:, :], spv[:, s, :], bk, True, True, ["spv%d" % (s // 2)] + SETUP)
            for pr in range(2):
                P.add("act", lambda e, pr=pr: e.activation(out=eb[:, pr, :], in_=pbk[pr][:, :], func=AF.Exp), reads=[PS(pr)], writes=["eb%d" % pr])
                P.add("act", lambda e, pr=pr: e.activation(out=enb[:, pr, :], in_=pbk[pr][:, :], func=AF.Exp, scale=-1.0), reads=[PS(pr)], writes=["enb%d" % pr])
                P.add("act", lambda e, pr=pr: e.activation(out=dec[:, pr, :], in_=pbk[pr][:, :].rearrange("p (c t) -> p c t", t=64)[:, :, 63], func=AF.Exp), reads=[PS(pr)], writes=["dec"])
            for hf in range(2):
                bk = 3 + hf
                P.add("act", lambda e, hf=hf, bk=bk: e.activation(out=edec[:, 2 * hf:2 * hf + 2, :], in_=pbk[bk][:, :].rearrange("p (s n) -> p s n", s=2), func=AF.Exp), reads=[PS(bk)], writes=["edec%d" % hf])

            slot, wv = wload_fm(win_bf, 0)
            for pr in range(2):
                bk = pr
                proj_fm(slot, wv, pr * 128, bk)
                P.add("dve", lambda e, pr=pr, bk=bk: e.scalar_tensor_tensor(out=qf[:, pr, :], in0=pbk[bk][:, :], scalar=0.125, in1=eb[:, pr, :], op0=ALU.mult, op1=ALU.mult), reads=[PS(bk), "eb%d" % pr], writes=["qf%d" % pr])
                P.add("dve", lambda e, pr=pr, bk=bk: e.scalar_tensor_tensor(out=qe[:, pr, :], in0=pbk[bk][:, :], scalar=0.125, in1=enb[:, pr, :], op0=ALU.mult, op1=ALU.mult), reads=[PS(bk), "enb%d" % pr], writes=["qe%d" % pr])
            slot, wv = wload_fm(win_bf, 256)
            for pr in range(2):
                bk = 3 + pr
                proj_fm(slot, wv, pr * 128, bk)
                P.add("dve", lambda e, pr=pr, bk=bk: e.tensor_tensor(out=ke[:, pr, :], in0=pbk[bk][:, :], in1=enb[:, pr, :], op=ALU.mult), reads=[PS(bk), "enb%d" % pr], writes=["ke%d" % pr])
                P.add("dve", lambda e, pr=pr, bk=bk: e.tensor_tensor(out=kb[:, pr, :], in0=pbk[bk][:, :], in1=eb[:, pr, :], op=ALU.mult), reads=[PS(bk), "eb%d" % pr], writes=["kb%d" % pr])
            for s in range(4):
                bk = 5 + (s % 2)
                proj_tm(slot, wv, 0, 256, s, bk)
                for par in range(2):
                    P.add("dve", lambda e, s=s, bk=bk, par=par: e.tensor_tensor(
                        out=kdecP[:, s, par::2, par * 64:par * 64 + 64],
                        in0=pbk[bk][:, 0:256].rearrange("p (h d) -> p h d", h=4)[:, par::2, :],
                        in1=edec[:, s, :].rearrange("p (h d) -> p h d", h=4)[:, par::2, :], op=ALU.mult),
                        reads=[PS(bk), "edec%d" % (s // 2)], writes=["kdecP%d" % s])

            if dbg and g == 0:
                P.add("pool", lambda e: e.dma_start(out=d_sp, in_=spv), reads=["spv0", "spv1"], chan="q_dbg")
                for n_, (a_, k_) in enumerate(((eb, "eb"), (qf, "qf"), (ke, "ke"), (enb, "enb"))):
                    P.add("pool", lambda e: e.dma_start(out=d_g[:, n_ * 2 * T:(n_ + 1) * 2 * T], in_=a_[:, :, :].rearrange("p a t -> p (a t)")), reads=[k_ + "0", k_ + "1"], chan="q_dbg")
                P.add("pool", lambda e: e.dma_start(out=d_gv, in_=gv_tok[:, :, :].rearrange("p a t -> p (a t)")), reads=["gv_tok%d" % k for k in range(4)], chan="q_dbg")
            phase("gla_core")
            AE, AO, OBE, OBO, DSE, DSO, NB = 0, 1, 2, 3, 4, 5, 6
            ob_store = [(Ub[0][:, 0:T], "Ub0"), (Ub[1][:, 0:T], "Ub1"), (fs[6], "fs6"), (fs[7], "fs7")]
            for pr in range(2):
                for s in range(4):
                    sl = slice(s * 128, (s + 1) * 128)
                    for par in range(2):
                        bk = AE if par == 0 else AO
                        rs = slice(par * 64, par * 64 + 64)
                        rk = ["ke%d" % pr, "qf%d" % pr, "kb%d" % pr, "qe%d" % pr]
                        mm(pbk[bk][:, 0:128], ke[rs, pr, sl], qf[rs, pr, sl], bk, True, True, rk)
                        mm(pbk[bk][:, 128:256], kb[rs, pr, sl], qe[rs, pr, sl], bk, True, True, rk)
                    ab = Abf[s % 2]
                    abk = "Abf%d" % (s % 2)
                    for par in range(2):
                        bk = AE if par == 0 else AO
                        i1, i2 = (s % 2) * 4 + par * 2, (s % 2) * 4 + par * 2 + 1
                        t1, t2 = fs[i1], fs[i2]
                        P.add("dve", lambda e, bk=bk, t1=t1: e.tensor_tensor(out=t1[:, 0:128], in0=pbk[bk][:, 0:128], in1=mf[:, :], op=ALU.mult), reads=[PS(bk)] + SETUP, writes=["fs%d" % i1])
                        P.add("dve", lambda e, bk=bk, t2=t2: e.tensor_tensor(out=t2[:, 0:128], in0=pbk[bk][:, 128:256], in1=mb[:, :], op=ALU.mult), reads=[PS(bk)] + SETUP, writes=["fs%d" % i2])
                        P.add("pool", lambda e, t1=t1, t2=t2, ab=ab, par=par: e.tensor_tensor(out=ab[:, par, :], in0=t1[:, 0:128], in1=t2[:, 0:128], op=ALU.add),
                              reads=["fs%d" % i1, "fs%d" % i2], writes=[abk + "_%d" % par])
                    for par in range(2):
                        hh = 2 * pr + par
                        bk = OBE if par == 0 else OBO
                        mm(pbk[bk][:, sl], gv_tok[:, s, hh * 128:(hh + 1) * 128], ab[:, par, :], bk, True, True, ["gv_tok%d" % s, abk + "_%d" % par])
                    for cc in range(2):
                        bk = DSE if cc == 0 else DSO
                        rs = slice(cc * 64, cc * 64 + 64)
                        for par in range(2):
                            hh = 2 * pr + par
                            mm(pbk[bk][:, par * 128:(par + 1) * 128], kdecP[rs, s, hh, :], gv_tok[rs, s, hh * 128:(hh + 1) * 128], bk, True, True, ["kdecP%d" % s, "gv_tok%d" % s])
                        c = 2 * s + cc
                        for par in range(2):
                            ps_ = slice(par * 64, par * 64 + 64)
                            P.add("dve", lambda e: e.scalar_tensor_tensor(out=S[ps_, (c + 1) % 2, pr, :], in0=S[ps_, c % 2, pr, :], scalar=dec[ps_, pr, c:c + 1], in1=pbk[bk][ps_, par * 128:(par + 1) * 128], op0=ALU.mult, op1=ALU.add),
                                  reads=["S%d_%d" % (pr, c % 2), "dec", PS(bk)], writes=["S%d_%d" % (pr, (c + 1) % 2)])
                        P.add("pool", lambda e, pr=pr, c=c: e.tensor_copy(out=Sbf[:, c + 1, pr, :], in_=S[:, (c + 1) % 2, pr, :]), reads=["S%d_%d" % (pr, (c + 1) % 2)], writes=["Sbf%d_%d" % (pr, c + 1)])
                for c in range(8):
                    for par in range(2):
                        bk = AE if par == 0 else AO
                        rs = slice(par * 64, par * 64 + 64)
                        mm(pbk[bk][:, c * 64:(c + 1) * 64], Sbf[rs, c, pr, :], qf[rs, pr, c * 64:(c + 1) * 64], bk, True, True, ["Sbf%d_%d" % (pr, c), "qf%d" % pr])
                P.add("pool", lambda e, pr=pr: e.tensor_copy(out=Sbf[:, 0, pr, :], in_=Sbf[:, 8, pr, :]), reads=["Sbf%d_8" % pr], writes=["Sbf%d_0" % pr])
                for par in range(2):
                    hh = 2 * pr + par
                    bk = OBE if par == 0 else OBO
                    ob, obk = ob_store[hh]
                    P.add("act", lambda e, bk=bk, ob=ob: e.copy(out=ob[:], in_=pbk[bk][:, :]), reads=[PS(bk)], writes=[obk])
                    bi = AE if par == 0 else AO
                    P.add("dve", lambda e: e.tensor_tensor(out=ob[:], in0=pbk[bi][:, :], in1=ob[:], op=ALU.add), reads=[PS(bi), obk], writes=[obk])

            def gla_epilogue(hh):
                ob, obk = ob_store[hh]
                head_norm(ob[:], obk, gnT[:, hh:hh + 1], oT[:, hh, :], ["oT%d" % hh], sg[:, hh, :], "sg%d" % hh, 0.5, 6,
                          fs[4 + hh % 2], "fs%d" % (4 + hh % 2), bs[4 + hh % 2], "bs%d" % (4 + hh % 2))

            phase("dqdk")
            for which, cbase in enumerate((1552, 2064)):
                for hd in range(4):
                    if hd % 2 == 0:
                        sl_, w_ = wload_fm(win_bf, cbase + 128 * hd)
                    bk = hd % 2
                    pbn = 4 + bk
                    proj_fm(sl_, w_, bk * 128, pbn)
                    zb = bs[2 + bk]
                    zk = "bs%d" % (2 + bk)
                    P.add("act", lambda e: e.copy(out=zb[:], in_=pbk[pbn][:, :]), reads=[PS(pbn)], writes=[zk])
                    rb = 7
                    mm(pbk[rb][:, :], perm_bf[:, :], zb[:], rb, True, True, [zk] + SETUP)
                    t1, t2 = fs[bk * 2], fs[bk * 2 + 1]
                    P.add("dve", lambda e, zb=zb, t1=t1: e.tensor_tensor(out=t1[:], in0=zb[:], in1=cosT[:], op=ALU.mult), reads=[zk, "cosT"], writes=["fs%d" % (bk * 2)])
                    P.add("dve", lambda e, rb=rb, t2=t2: e.tensor_tensor(out=t2[:], in0=pbk[rb][:, :], in1=sinT[:], op=ALU.mult), reads=[PS(rb), "sinT"], writes=["fs%d" % (bk * 2 + 1)])
                    if which == 0:
                        dst, dk_ = qT[:, hd, :], "qT%d" % hd
                    else:
                        dst, dk_ = KT[:, hd, i * T:(i + 1) * T], "KT%d" % hd
                    P.add("pool", lambda e, t1=t1, t2=t2, dst=dst: e.tensor_tensor(out=dst, in0=t1[:], in1=t2[:], op=ALU.add),
                          reads=["fs%d" % (bk * 2), "fs%d" % (bk * 2 + 1)], writes=[dk_])
                    if hd % 2 == 1:
                        gla_epilogue(which * 2 + hd // 2)
            for half in range(2):
                slot, wv4 = wload_tm(win_bf, 2576, 4 * half)
                for s in range(4):
                    tm_half(slot, wv4, 4 * half, s, s)
            for s in range(4):
                P.add("act", lambda e: e.copy(out=V[:, i * 4 + s, :], in_=pbk[s][:, :]), reads=[PS(s)], writes=["V"])

            phase("attn")
            SAb, SBb, OA0, OA1, ZA0, ZA1 = (0, 2), (1, 3), 4, 5, 6, 7
            cnt = 0
            deferred = []

            def attn_deferred():
                while deferred:
                    hd_ = deferred.pop(0)
                    P.add("dve", lambda e: e.tensor_tensor(out=fs[6][:], in0=fs[1][:], in1=fs[4][:], op=ALU.mult), reads=["fs1", "fs4"], writes=["fs6"])
                    P.add("dve", lambda e: e.tensor_tensor(out=fs[5][:], in0=fs[2][:], in1=fs[5][:], op=ALU.mult), reads=["fs2", "fs5"], writes=["fs5"])
                    P.add("dve", lambda e: e.scalar_tensor_tensor(out=fs[7][:], in0=fs[5][:], scalar=nlam[:, 0:1], in1=fs[6][:], op0=ALU.mult, op1=ALU.add), reads=["fs5", "fs6", "nlam"], writes=["fs7"])
                    head_norm(fs[7][:], "fs7", dn8[:, hd_:hd_ + 1], oT[:, 4 + hd_, :], ["oT%d" % (4 + hd_)], None, None, None, ZA0,
                              fs[4], "fs4", bs[4], "bs4")

            for hd in range(4):
                ktiles = [(j, 0) for j in range(4 * i)] + [(4 * i + kt, 128 * kt) for kt in range(4)]
                if i == 0:
                    pass
                nk = len(ktiles)
                pend = None
                pacc = (fs[0], fs[1])
                P.add("pool", lambda e: e.memset(pacc[0][:], 0.0), writes=["fs0"])
                for n in range(nk + 1):
                    cur = None
                    if n < nk:
                        j, c0 = ktiles[n]
                        pp = cnt % 2
                        cnt += 1
                        cs = slice(c0, T)
                        ks = slice(j * 128, (j + 1) * 128)
                        mm(pbk[SAb[pp]][:, cs], KT[0:64, hd, ks], qT[0:64, hd, cs], SAb[pp], True, True, ["KT%d" % hd, "qT%d" % hd])
                        mm(pbk[SBb[pp]][:, cs], KT[64:128, hd, ks], qT[64:128, hd, cs], SBb[pp], True, True, ["KT%d" % hd, "qT%d" % hd])
                        P0, P1 = bs[2 * pp], bs[2 * pp + 1]
                        k0, k1 = "bs%d" % (2 * pp), "bs%d" % (2 * pp + 1)
                        P.add("act", lambda e: e.activation(out=Pb[pp][:, :, cs], in_=pb2[pp][:, :].rearrange("p (m t) -> p m t", m=2)[:, :, cs], func=AF.Exp, scale=0.125),
                              reads=[PS(SAb[pp]), PS(SBb[pp])], writes=[k0, k1])
                        if j >= 4 * i:
                            P.add("pool", lambda e: e.memset(Pb[pp][64:128, :, c0:c0 + 64], 0.0), writes=[k0, k1])
                        P.add("dve", lambda e: e.tensor_tensor(out=pacc[0][:, cs], in0=pacc[0][:, cs], in1=P0[:, cs], op=ALU.add), reads=[k0, "fs0"], writes=["fs0"])
                        cur = (n, j, cs, P0, P1, k0, k1)
                    if pend is not None:
                        (n_, j_, cs_, Q0, Q1, q0, q1) = pend
                        st, sp_ = (n_ == 0), (n_ == nk - 1)
                        vv = V[:, j_, hd * 128:(hd + 1) * 128]
                        mm(pbk[OA0][:, cs_], vv, Q0[:, cs_], OA0, st, sp_, ["V", q0])
                        mm(pbk[OA1][:, cs_], vv, Q1[:, cs_], OA1, st, sp_, ["V", q1])
                        mm(pbk[ZA1][:, cs_], ones_bf[:, :], Q1[:, cs_], ZA1, st, sp_, ["ones_bf", q1])
                    pend = cur
                    if n == 2:
                        attn_deferred()
                mm(pbk[ZA0][:, :], ones_f[:, :], pacc[0][:], ZA0, True, True, ["ones_f", "fs0"])
                P.add("act", lambda e: e.copy(out=fs[1][:], in_=pbk[OA0][:, :]), reads=[PS(OA0)], writes=["fs1"])
                P.add("act", lambda e: e.copy(out=fs[2][:], in_=pbk[OA1][:, :]), reads=[PS(OA1)], writes=["fs2"])
                P.add("act", lambda e: e.activation(out=fs[4][:], in_=pbk[ZA0][:, :], func=AF.Ln), reads=[PS(ZA0)], writes=["fs4"])
                P.add("act", lambda e: e.activation(out=fs[5][:], in_=pbk[ZA1][:, :], func=AF.Ln), reads=[PS(ZA1)], writes=["fs5"])
                P.add("act", lambda e: e.activation(out=fs[4][:], in_=fs[4][:], func=AF.Exp, scale=-1.0), reads=["fs4"], writes=["fs4"])
                P.add("act", lambda e: e.activation(out=fs[5][:], in_=fs[5][:], func=AF.Exp, scale=-1.0), reads=["fs5"], writes=["fs5"])
                deferred.append(hd)
            attn_deferred()

            if dbg and g == 0:
                P.add("pool", lambda e: e.dma_start(out=d_oT, in_=oTf[:, :]), reads=["oT%d" % k for k in range(8)], chan="q_dbg")
                P.add("pool", lambda e: e.dma_start(out=d_qk[:, 0:4 * T], in_=qT[:, :, :].rearrange("p a t -> p (a t)")), reads=["qT%d" % k for k in range(4)], chan="q_dbg")
                P.add("pool", lambda e: e.dma_start(out=d_qk[:, 4 * T:5 * T], in_=KT[:, 0, 0:T]), reads=["KT0"], chan="q_dbg")
            phase("wout")
            for hf in range(2):
                for half in range(2):
                    slot, wv4 = wload_tm(wout_bf, hf * 512, 4 * half)
                    for s in range(4):
                        tm_half(slot, wv4, 4 * half, s, hf * 4 + s, act=oT, akeys="oT%d")
                for s in range(4):
                    bk = hf * 4 + s
                    P.add("dve", lambda e, s=s, hf=hf, bk=bk: e.tensor_tensor(out=h[:, s, hf * 512:(hf + 1) * 512], in0=pbk[bk][:, :], in1=h[:, s, hf * 512:(hf + 1) * 512], op=ALU.add),
                          reads=[PS(bk), "h%d" % s], writes=["h%d" % s])

            if dbg and g == 0:
                P.add("sp", lambda e: e.dma_start(out=d_h, in_=h[:, :, :]), reads=["h0", "h1", "h2", "h3"], chan="q_dbg2")
            phase("ffn_norm")
            norm_T(gT2, "gT2")
            phase("ffn_up")
            SQC = math.sqrt(0.044715)
            blocks = {}

            def ffn_A(j):
                gb, q = j // 2, j % 2
                if q == 0:
                    blocks[gb] = (wload_fm(wup_bf, gb * 256), wload_fm(wup_bf, DFF + gb * 256))
                for which in range(2):
                    sl_, w_ = blocks[gb][which]
                    bk = (j % 2) * 2 + which
                    proj_fm(sl_, w_, q * 128, bk)

            def ffn_B(j):
                for which in range(2):
                    ch = j + 22 * which
                    bk = (j % 2) * 2 + which
                    ub = Ub[which]
                    uk = "Ub%d" % which
                    fi = (j % 3) * 2 + which
                    cbuf, ck = fs[fi], "fs%d" % fi
                    P.add("pool", lambda e: e.tensor_copy(out=ub[:, 0:2], in_=HB[:, ch, :]), reads=["HB%d" % ch, "HB"], writes=[uk])
                    P.add("act", lambda e: e.copy(out=ub[:, 2:T + 2], in_=pbk[bk][:, :]), reads=[PS(bk)], writes=[uk])
                    P.add("act", lambda e: e.activation(out=cbuf[:], in_=pbk[bk][:, :], func=AF.Identity, scale=cwT[:, 2, ch:ch + 1], bias=cbT[:, ch:ch + 1]),
                          reads=[PS(bk), "cwh"] + SETUP, writes=[ck])
                    P.add("dve", lambda e: e.scalar_tensor_tensor(out=cbuf[:], in0=ub[:, 1:T + 1], scalar=cwT[:, 1, ch:ch + 1], in1=cbuf[:], op0=ALU.mult, op1=ALU.add),
                          reads=[uk, ck, "cwh"] + SETUP, writes=[ck])
                    P.add("dve", lambda e: e.scalar_tensor_tensor(out=cbuf[:], in0=ub[:, 0:T], scalar=cwT[:, 0, ch:ch + 1], in1=cbuf[:], op0=ALU.mult, op1=ALU.add),
                          reads=[uk, ck, "cwh"] + SETUP, writes=[ck])
                    P.add("pool", lambda e: e.tensor_copy(out=HB[:, ch, :], in_=ub[:, T:T + 2]), reads=[uk], writes=["HB%d" % ch])

            def ffn_bufs(j):
                base = (j % 3) * 2
                ti = 6 + (j % 2)
                return (fs[base], fs[base + 1], fs[ti], "fs%d" % base, "fs%d" % (base + 1), "fs%d" % ti)

            def ffn_C1(j):
                cg, cv, tmp, kcg, kcv, ktm = ffn_bufs(j)
                P.add("act", lambda e: e.activation(out=tmp[:], in_=cg[:], func=AF.Square, scale=SQC), reads=[kcg], writes=[ktm])
                P.add("dve", lambda e: e.scalar_tensor_tensor(out=tmp[:], in0=tmp[:], scalar=1.0, in1=cg[:], op0=ALU.add, op1=ALU.mult), reads=[ktm, kcg], writes=[ktm])
                P.add("pool", lambda e: e.tensor_tensor(out=cv[:], in0=cg[:], in1=cv[:], op=ALU.mult), reads=[kcg, kcv], writes=[kcv])

            def ffn_C2(j):
                cg, cv, tmp, kcg, kcv, ktm = ffn_bufs(j)
                P.add("act", lambda e: e.activation(out=tmp[:], in_=tmp[:], func=AF.Tanh, scale=0.7978845608028654), reads=[ktm], writes=[ktm])
                P.add("dve", lambda e: e.scalar_tensor_tensor(out=actT[:, j, :], in0=tmp[:], scalar=1.0, in1=cv[:], op0=ALU.add, op1=ALU.mult), reads=[ktm, kcv], writes=["actT%d" % j])

            wdv = wdn_bf.rearrange("(c p) n -> p c n", p=128)
            dblk = {}

            def ffn_D(j):
                if j % 2 == 0:
                    dblk[0] = wload(lambda r: r[:, 0:2048].rearrange("p (c n) -> p c n", c=2), wdv[:, j:j + 2, :])
                slot, wv = dblk[0]
                for s in range(2):
                    for hf in range(2):
                        bk = 4 + s * 2 + hf
                        mm(pbk[bk][:, :], actT[:, j, s * 128:(s + 1) * 128], wv[:, j % 2, hf * 512:(hf + 1) * 512], bk, j == 0, j == 21, [("ring", slot), "actT%d" % j])

            ffn_A(0)
            for t in range(25):
                if t + 1 < 22:
                    ffn_A(t + 1)
                if t < 22:
                    ffn_B(t)
                if 0 <= t - 1 < 22:
                    ffn_C1(t - 1)
                if 0 <= t - 2 < 22:
                    ffn_C2(t - 2)
                if 0 <= t - 3 < 22:
                    ffn_D(t - 3)

            def wd_evac(s, bk, hf):
                P.add("dve", lambda e: e.tensor_tensor(out=h[:, s, hf * 512:(hf + 1) * 512], in0=pbk[bk][:, :], in1=h[:, s, hf * 512:(hf + 1) * 512], op=ALU.add),
                      reads=[PS(bk), "h%d" % s], writes=["h%d" % s])

            phase("w_down")
            for s in range(2):
                for hf in range(2):
                    wd_evac(s, 4 + s * 2 + hf, hf)
            for bl in range(11):
                slot, wv = wload(lambda r: r[:, 0:2048].rearrange("p (c n) -> p c n", c=2), wdv[:, bl * 2:bl * 2 + 2, :])
                for fl in range(2):
                    f = bl * 2 + fl
                    for s in (2, 3):
                        for hf in range(2):
                            bk = (s - 2) * 2 + hf
                            mm(pbk[bk][:, :], actT[:, f, s * 128:(s + 1) * 128], wv[:, fl, hf * 512:(hf + 1) * 512], bk, f == 0, f == 21, [("ring", slot), "actT%d" % f])
            for s in (2, 3):
                for hf in range(2):
                    wd_evac(s, (s - 2) * 2 + hf, hf)

            phase("ple")
            norm_T(gT3, "gT3")
            for s in range(4):
                tb = 6 + (s % 2)
                tv = trv[s % 2]
                for kc in range(2):
                    P.add("pe", lambda e, kc=kc, s=s, tv=tv: e.transpose(tv[:, kc * 128:(kc + 1) * 128], p_bf[:, s, kc * 128:(kc + 1) * 128], ident_bf[:]),
                          reads=["p_bf"] + SETUP, writes=[PS(tb)])
                P.add("dve", lambda e, s=s, tv=tv: e.tensor_copy(out=pT[:, :, s * 128:(s + 1) * 128], in_=tv[:, 0:256].rearrange("p (kc t) -> p kc t", kc=2)), reads=[PS(tb)], writes=["qT0", "qT1"])
            for hf in range(2):
                for half in range(2):
                    slot, wv4 = wload_tm(wpg_bf, hf * 512, 4 * half)
                    for s in range(4):
                        tm_half(slot, wv4, 4 * half, s, s)
                for s in range(4):
                    bg = s
                    bp = 4 + (s % 2)
                    for kc in range(2):
                        mm(pbk[bp][:, :], pT[:, kc, s * 128:(s + 1) * 128], wpp[:, kc, hf * 512:(hf + 1) * 512], bp, kc == 0, kc == 1, ["qT0", "qT1"] + SETUP)
                    th = fs[s]
                    tk = "fs%d" % s
                    P.add("act", lambda e: e.activation(out=th[:], in_=pbk[bg][:, :], func=AF.Tanh, scale=0.5), reads=[PS(bg)], writes=[tk])
                    P.add("dve", lambda e: e.scalar_tensor_tensor(out=th[:], in0=th[:], scalar=1.0, in1=pbk[bp][:, :], op0=ALU.add, op1=ALU.mult), reads=[tk, PS(bp)], writes=[tk])
                    P.add("dve", lambda e: e.scalar_tensor_tensor(out=h[:, s, hf * 512:(hf + 1) * 512], in0=th[:], scalar=0.5, in1=h[:, s, hf * 512:(hf + 1) * 512], op0=ALU.mult, op1=ALU.add),
                          reads=[tk, "h%d" % s], writes=["h%d" % s])

            phase("final")
            P.add("pool", lambda e: e.memset(ssq[:], 0.0), writes=["ssq"])
            for s in range(4):
                P.add("act", lambda e, s=s: e.activation(out=junk[:], in_=h[:, s, :], func=AF.Square, accum_out=ssq[:, s:s + 1]), reads=["h%d" % s, "ssq"], writes=["hn_tok1", "ssq"])
            rstd_from(ssq[:], rstd[:], 4, 1.0 / 1024.0, ["ssq"], "rstd", lnv[:], "lnv")
            for s in range(4):
                P.add("dve", lambda e, s=s: e.scalar_tensor_tensor(out=h[:, s, :], in0=h[:, s, :], scalar=rstd[:, s:s + 1], in1=gf_bc[:, :], op0=ALU.mult, op1=ALU.mult),
                      reads=["h%d" % s, "rstd"] + SETUP, writes=["h%d" % s])
                P.add("sp", lambda e, s=s: e.dma_start(out=y_d[tok0 + s * 128: tok0 + (s + 1) * 128, :], in_=h[:, s, :]), reads=["h%d" % s], chan="q_y%d" % s)

        P.wait_all("sp", ["q_y0", "q_y1", "q_y2", "q_y3", "q_dbg", "q_dbg2"])
        P.emit()
    return nc


_W_KEYS = ["w_in", "w_out", "w_up", "w_down", "w_ple_gate", "w_ple_proj", "w_a_up"]
_V_KEYS = ["b_a", "norm_mix", "norm_ffn", "norm_ple", "gla_norm", "diff_norm", "conv_b",
           "lam_q1", "lam_k1", "lam_q2", "lam_k2"]


def kernel(**inputs):
    x = np.asarray(inputs["x"], dtype=np.float32)
    p = np.asarray(inputs["p"], dtype=np.float32)[0]
    pos = np.asarray(inputs["positions"]).astype(np.int32)
    shared = {}
    for k in _W_KEYS:
        shared[k] = np.ascontiguousarray(np.asarray(inputs[k], dtype=np.float32)[0])
    for k in _V_KEYS:
        shared[k] = np.ascontiguousarray(np.asarray(inputs[k], dtype=np.float32).reshape(1, -1))
    shared["conv_w"] = np.ascontiguousarray(np.asarray(inputs["conv_w"], dtype=np.float32)[0])
    shared["norm_final"] = np.ascontiguousarray(np.asarray(inputs["norm_final"], dtype=np.float32).reshape(1, -1))
    shared.update(host_consts())
    in_maps = []
    for c in range(NCORES):
        m = dict(shared)
        m["x"] = np.ascontiguousarray(x[c * NSEQ:(c + 1) * NSEQ].reshape(NSEQ * SEQ, 1024))
        m["p"] = np.ascontiguousarray(p[c * NSEQ:(c + 1) * NSEQ].reshape(NSEQ * SEQ, 256))
        m["pos"] = np.ascontiguousarray(pos[c * NSEQ:(c + 1) * NSEQ].reshape(1, NSEQ * SEQ))
        in_maps.append(m)
    nc = build_program()
    res = run_bass_kernel_spmd(nc, in_maps, core_ids=list(range(NCORES)))
    out = np.concatenate([np.asarray(r["y"]).reshape(NSEQ, SEQ, 1024) for r in res.results], axis=0)
    return out.astype(np.float32)
```
